# Optimizing a Trainium2 kernel written in Bass

```python
import math
import jax
import jax.numpy as jnp
from jax import lax
import numpy as np

D_MODEL = 1024
BATCH = 4
SEQ = 8192
DEPTH = 4

GRID_W = 64
CTX_LEN = 256
N_MIXERS = 3
N_GDN_LAYERS = (DEPTH + 2) // 3
N_LRU_LAYERS = (DEPTH + 1) // 3
N_HG_LAYERS = DEPTH // 3
EPS = 1e-6
CONV_W = 4
CONV_PAD = (2, 1)
FFN_HIDDEN = -(-(8 * D_MODEL) // (3 * 256)) * 256

GDN_HEADS = 8
GDN_DK = 128
GDN_DV = 128
GDN_CHUNK = 64
GDN_QKV = GDN_HEADS * (2 * GDN_DK + GDN_DV)
GDN_Z = GDN_HEADS * GDN_DV
GDN_IN = GDN_QKV + GDN_Z + 4 * GDN_HEADS

LRU_WIDTH = D_MODEL
LRU_BLOCKS = 8
LRU_BW = LRU_WIDTH // LRU_BLOCKS
LRU_C = 8.0

HG_HEADS = 8
HG_DK = 128
HG_DV = D_MODEL // HG_HEADS
HG_CHUNK = 64
HG_QK = HG_HEADS * HG_DK
HG_V = HG_HEADS * HG_DV
HG_IN = 3 * HG_QK + 2 * HG_V

kernel_name = 'hybrid_gdn_rglru_hgrn2_flow_block'


def rmsnorm(x, w):
    xf = x.astype(jnp.float32)
    y = xf * lax.rsqrt(jnp.mean(xf * xf, axis=-1, keepdims=True) + EPS)
    return (y * w.astype(jnp.float32)).astype(x.dtype)


def l2norm(x):
    xf = x.astype(jnp.float32)
    return xf * lax.rsqrt(jnp.sum(xf * xf, axis=-1, keepdims=True) + EPS)


def modulate(h, shift, scale):
    return h * (1.0 + scale) + shift


def short_conv(x, w, b=None):
    L = x.shape[1]
    xp = jnp.pad(x, ((0, 0), CONV_PAD, (0, 0)))
    y = xp[:, 0:L] * w[0]
    for j in range(1, CONV_W):
        y = y + xp[:, j:j + L] * w[j]
    return y if b is None else y + b


def to_heads(t, n_heads):
    bsz, L, _ = t.shape
    return t.reshape(bsz, L, n_heads, -1).transpose(0, 2, 1, 3)


def flip(t, rev, axis):
    return jnp.flip(t, axis=axis) if rev else t


def to_scan_order(h, col):
    if not col:
        return h
    bsz, L, d = h.shape
    rows = L // GRID_W
    return h.reshape(bsz, rows, GRID_W, d).transpose(0, 2, 1, 3).reshape(bsz, L, d)


def from_scan_order(h, col):
    if not col:
        return h
    bsz, L, d = h.shape
    rows = L // GRID_W
    return h.reshape(bsz, GRID_W, rows, d).transpose(0, 2, 1, 3).reshape(bsz, L, d)


def gated_head_norm(o, z, w):
    bsz, nh, L, dv = o.shape
    y = rmsnorm(o.transpose(0, 2, 1, 3), w) * jax.nn.silu(z.reshape(bsz, L, nh, dv).astype(jnp.float32))
    return y.reshape(bsz, L, nh * dv).astype(z.dtype)


def swiglu(h, w1, w3, w2):
    return (jax.nn.silu(h @ w1) * (h @ w3)) @ w2


def gated_delta_rule(q, k, v, g, beta, s0):
    bsz, nh, L, dk = q.shape
    dv = v.shape[-1]
    C = GDN_CHUNK
    n = L // C
    q, k = (t.astype(jnp.float32).reshape(bsz, nh, n, C, dk) for t in (q, k))
    v = v.astype(jnp.float32).reshape(bsz, nh, n, C, dv)
    g, beta = (t.astype(jnp.float32).reshape(bsz, nh, n, C) for t in (g, beta))
    gcum = jnp.cumsum(g, axis=-1)
    incl = jnp.tril(jnp.ones((C, C), dtype=bool))
    strict = jnp.tril(jnp.ones((C, C), dtype=bool), -1)
    decay = jnp.exp(jnp.where(incl, gcum[..., :, None] - gcum[..., None, :], -jnp.inf))
    kb = k * beta[..., None]
    lmat = jnp.where(strict, jnp.einsum('bhnid,bhnjd->bhnij', kb, k) * decay, 0.0)
    eye = jnp.eye(C, dtype=jnp.float32)
    t_inv = lax.linalg.triangular_solve(eye + lmat, jnp.broadcast_to(eye, lmat.shape),
                                        left_side=True, lower=True, unit_diagonal=True)
    u = t_inv @ (v * beta[..., None])
    w = t_inv @ (kb * jnp.exp(gcum)[..., None])
    a_qk = jnp.einsum('bhnid,bhnjd->bhnij', q, k) * decay
    glast = gcum[..., -1]
    q_dec = q * jnp.exp(gcum)[..., None]
    k_dec = k * jnp.exp(glast[..., None] - gcum)[..., None]
    xs = tuple(jnp.moveaxis(t, 2, 0) for t in (q_dec, k_dec, u, w, a_qk, jnp.exp(glast)))

    def step(s, inp):
        qd, kd, uc, wc, aqk, dl = inp
        v_new = uc - wc @ s
        o = qd @ s + aqk @ v_new
        s = s * dl[..., None, None] + jnp.swapaxes(kd, -1, -2) @ v_new
        return s, o

    s_fin, o = lax.scan(step, s0, xs)
    return jnp.moveaxis(o, 0, 2).reshape(bsz, nh, L, dv), s_fin


def gdn_project(h, w_in, conv_w, a_log, dt_bias):
    bsz, L, _ = h.shape
    p = h @ w_in
    qkv = jax.nn.silu(short_conv(p[..., :GDN_QKV], conv_w))
    z = p[..., GDN_QKV:GDN_QKV + GDN_Z]
    ab = p[..., GDN_QKV + GDN_Z:].astype(jnp.float32).reshape(bsz, L, 2, 2, GDN_HEADS)
    hk = GDN_HEADS * GDN_DK
    q = l2norm(to_heads(qkv[..., :hk], GDN_HEADS)) * (GDN_DK ** -0.5)
    k = l2norm(to_heads(qkv[..., hk:2 * hk], GDN_HEADS))
    v = to_heads(qkv[..., 2 * hk:], GDN_HEADS).astype(jnp.float32)
    g = -jnp.exp(a_log.astype(jnp.float32)) * jax.nn.softplus(ab[:, :, 0] + dt_bias.astype(jnp.float32))
    beta = jax.nn.sigmoid(ab[:, :, 1])
    return q, k, v, g.transpose(2, 0, 3, 1), beta.transpose(2, 0, 3, 1), z


def gdn_mixer(hc, hl, w_in, conv_w, a_log, dt_bias, norm_w, w_out, need_ctx):
    qc, kc, vc, gam_c, bet_c, zc = gdn_project(hc, w_in, conv_w, a_log, dt_bias)
    ql, kl, vl, gam_l, bet_l, zl = gdn_project(hl, w_in, conv_w, a_log, dt_bias)
    s0 = jnp.zeros((hc.shape[0], GDN_HEADS, GDN_DK, GDN_DV), jnp.float32)
    o_c, o_l = [], []
    for d in range(2):
        rev = d == 1
        oc, s_ctx = gated_delta_rule(*[flip(t, rev, 2) for t in (qc, kc, vc, gam_c[d], bet_c[d])], s0)
        ol, _ = gated_delta_rule(*[flip(t, rev, 2) for t in (ql, kl, vl, gam_l[d], bet_l[d])], s_ctx)
        o_c.append(flip(oc, rev, 2))
        o_l.append(flip(ol, rev, 2))
    out_l = gated_head_norm(o_l[0] + o_l[1], zl, norm_w) @ w_out
    out_c = gated_head_norm(o_c[0] + o_c[1], zc, norm_w) @ w_out if need_ctx else None
    return out_c, out_l


def linear_scan(log_a, b, h0):
    def combine(e1, e2):
        la1, b1 = e1
        la2, b2 = e2
        return la1 + la2, jnp.exp(la2) * b1 + b2
    la_cum, h = lax.associative_scan(combine, (log_a, b), axis=1)
    h = h + jnp.exp(la_cum) * h0[:, None, :]
    return h, h[:, -1]


def lru_gates(xr, w_r, b_r, w_i, b_i, lam):
    bsz, L, _ = xr.shape
    xf = xr.astype(jnp.float32)
    xb = xf.reshape(bsz, L, LRU_BLOCKS, LRU_BW)
    r = jax.nn.sigmoid(jnp.einsum('blgi,gij->blgj', xb, w_r.astype(jnp.float32)).reshape(bsz, L, LRU_WIDTH) + b_r)
    ig = jax.nn.sigmoid(jnp.einsum('blgi,gij->blgj', xb, w_i.astype(jnp.float32)).reshape(bsz, L, LRU_WIDTH) + b_i)
    log_a = -LRU_C * r * jax.nn.softplus(-lam.astype(jnp.float32))
    b = jnp.sqrt(-jnp.expm1(2.0 * log_a)) * (ig * xf)
    return log_a, b


def rglru_mixer(hc, hl, w_in, conv_w, conv_b, w_r, b_r, w_i, b_i, lam, w_out, need_ctx):
    def branches(h):
        p = h @ w_in
        return jax.nn.gelu(p[..., :LRU_WIDTH]), short_conv(p[..., LRU_WIDTH:], conv_w, conv_b)
    gate_c, xc = branches(hc)
    gate_l, xl = branches(hl)
    h0 = jnp.zeros((hc.shape[0], LRU_WIDTH), jnp.float32)
    h_c, h_l = [], []
    for d in range(2):
        rev = d == 1
        la, b = lru_gates(xc, w_r[d], b_r[d], w_i[d], b_i[d], lam[d])
        yc, s_ctx = linear_scan(flip(la, rev, 1), flip(b, rev, 1), h0)
        la, b = lru_gates(xl, w_r[d], b_r[d], w_i[d], b_i[d], lam[d])
        yl, _ = linear_scan(flip(la, rev, 1), flip(b, rev, 1), s_ctx)
        h_c.append(flip(yc, rev, 1))
        h_l.append(flip(yl, rev, 1))
    out_l = (gate_l * (h_l[0] + h_l[1]).astype(hl.dtype)) @ w_out
    out_c = (gate_c * (h_c[0] + h_c[1]).astype(hc.dtype)) @ w_out if need_ctx else None
    return out_c, out_l


def gla_chunk(q, k, log_f, v, s0):
    bsz, nh, L, dk = q.shape
    dv = v.shape[-1]
    C = HG_CHUNK
    n = L // C
    q, k, log_f = (t.reshape(bsz, nh, n, C, dk) for t in (q, k, log_f))
    v = v.reshape(bsz, nh, n, C, dv)
    gcum = jnp.cumsum(log_f, axis=-2)
    ref = gcum[..., C // 2:C // 2 + 1, :]
    incl = jnp.tril(jnp.ones((C, C), dtype=bool))
    scores = jnp.einsum('bhnid,bhnjd->bhnij', q * jnp.exp(gcum - ref), k * jnp.exp(ref - gcum))
    o_intra = jnp.where(incl, scores, 0.0) @ v
    glast = gcum[..., -1, :]
    upd = jnp.einsum('bhncd,bhnce->bhnde', k * jnp.exp(glast[..., None, :] - gcum), v)

    def step(s, inp):
        dl, u = inp
        return s * dl[..., :, None] + u, s

    s_fin, s_prev = lax.scan(step, s0, (jnp.moveaxis(jnp.exp(glast), 2, 0), jnp.moveaxis(upd, 2, 0)))
    o_inter = jnp.einsum('bhncd,bhnde->bhnce', q * jnp.exp(gcum), jnp.moveaxis(s_prev, 0, 2))
    return (o_intra + o_inter).reshape(bsz, nh, L, dv), s_fin


def hgrn2_project(h, w_in, lb):
    p = h @ w_in
    q = to_heads(jax.nn.silu(p[..., :HG_QK]), HG_HEADS).astype(jnp.float32)
    v = to_heads(p[..., 3 * HG_QK:3 * HG_QK + HG_V], HG_HEADS).astype(jnp.float32)
    z = p[..., 3 * HG_QK + HG_V:]
    keys, logfs = [], []
    for d in range(2):
        f = lb + (1.0 - lb) * jax.nn.sigmoid(p[..., (1 + d) * HG_QK:(2 + d) * HG_QK].astype(jnp.float32))
        keys.append(to_heads(1.0 - f, HG_HEADS))
        logfs.append(to_heads(jnp.log(f), HG_HEADS))
    return q, v, z, keys, logfs


def hgrn2_mixer(hc, hl, w_in, lb, norm_w, w_out, need_ctx):
    qc, vc, zc, kc, fc = hgrn2_project(hc, w_in, lb)
    ql, vl, zl, kl, fl = hgrn2_project(hl, w_in, lb)
    s0 = jnp.zeros((hc.shape[0], HG_HEADS, HG_DK, HG_DV), jnp.float32)
    o_c, o_l = [], []
    for d in range(2):
        rev = d == 1
        oc, s_ctx = gla_chunk(*[flip(t, rev, 2) for t in (qc, kc[d], fc[d], vc)], s0)
        ol, _ = gla_chunk(*[flip(t, rev, 2) for t in (ql, kl[d], fl[d], vl)], s_ctx)
        o_c.append(flip(oc, rev, 2))
        o_l.append(flip(ol, rev, 2))
    out_l = gated_head_norm(o_l[0] + o_l[1], zl, norm_w) @ w_out
    out_c = gated_head_norm(o_c[0] + o_c[1], zc, norm_w) @ w_out if need_ctx else None
    return out_c, out_l


def setup_inputs(seed: int = 0) -> dict:
    key = jax.random.key(seed)
    keys = iter(jax.random.split(key, 48))
    f32 = jnp.float32

    def normal(shape, scale=1.0):
        return scale * jax.random.normal(next(keys), shape, f32)

    def dense(shape, fan_in, scale=1.0):
        return normal(shape, scale * fan_in ** -0.5)

    def gain(shape):
        return 1.0 + normal(shape, 0.05)

    def uniform(shape, lo, hi):
        return jax.random.uniform(next(keys), shape, f32, lo, hi)

    D = D_MODEL
    dt = jnp.exp(uniform((N_GDN_LAYERS, 2, GDN_HEADS), math.log(1e-3), math.log(1e-1)))
    a0 = uniform((N_LRU_LAYERS, 2, LRU_WIDTH), 0.9, 0.999)
    return {
        'x': normal((BATCH, SEQ, D)),
        'c': normal((BATCH, D)),
        'ctx': normal((BATCH, CTX_LEN, D)),
        'c_ctx': normal((D,)),
        'ada_w': dense((DEPTH, D, 6 * D), D, 0.5),
        'ada_b': normal((DEPTH, 6 * D), 0.01),
        'norm_mix': gain((DEPTH, D)),
        'norm_ffn': gain((DEPTH, D)),
        'norm_final': gain((D,)),
        'ffn_w1': dense((DEPTH, D, FFN_HIDDEN), D),
        'ffn_w3': dense((DEPTH, D, FFN_HIDDEN), D),
        'ffn_w2': dense((DEPTH, FFN_HIDDEN, D), FFN_HIDDEN),
        'gdn_w_in': dense((N_GDN_LAYERS, D, GDN_IN), D),
        'gdn_conv': dense((N_GDN_LAYERS, CONV_W, GDN_QKV), CONV_W),
        'gdn_a_log': jnp.log(uniform((N_GDN_LAYERS, 2, GDN_HEADS), 1.0, 16.0)),
        'gdn_dt_bias': dt + jnp.log(-jnp.expm1(-dt)),
        'gdn_norm': gain((N_GDN_LAYERS, GDN_DV)),
        'gdn_w_out': dense((N_GDN_LAYERS, GDN_Z, D), GDN_Z),
        'lru_w_in': dense((N_LRU_LAYERS, D, 2 * LRU_WIDTH), D),
        'lru_conv_w': dense((N_LRU_LAYERS, CONV_W, LRU_WIDTH), CONV_W),
        'lru_conv_b': normal((N_LRU_LAYERS, LRU_WIDTH), 0.01),
        'lru_w_r': dense((N_LRU_LAYERS, 2, LRU_BLOCKS, LRU_BW, LRU_BW), LRU_BW),
        'lru_b_r': normal((N_LRU_LAYERS, 2, LRU_WIDTH), 0.01),
        'lru_w_i': dense((N_LRU_LAYERS, 2, LRU_BLOCKS, LRU_BW, LRU_BW), LRU_BW),
        'lru_b_i': normal((N_LRU_LAYERS, 2, LRU_WIDTH), 0.01),
        'lru_lambda': jnp.log(a0) - jnp.log1p(-a0),
        'lru_w_out': dense((N_LRU_LAYERS, LRU_WIDTH, D), LRU_WIDTH),
        'hg_w_in': dense((N_HG_LAYERS, D, HG_IN), D),
        'hg_lb_logits': normal((DEPTH, HG_QK), 0.1),
        'hg_norm': gain((N_HG_LAYERS, HG_DV)),
        'hg_w_out': dense((N_HG_LAYERS, HG_V, D), HG_V),
    }


def reference(x, c, ctx, c_ctx, ada_w, ada_b, norm_mix, norm_ffn, norm_final,
              ffn_w1, ffn_w3, ffn_w2,
              gdn_w_in, gdn_conv, gdn_a_log, gdn_dt_bias, gdn_norm, gdn_w_out,
              lru_w_in, lru_conv_w, lru_conv_b, lru_w_r, lru_b_r, lru_w_i, lru_b_i,
              lru_lambda, lru_w_out,
              hg_w_in, hg_lb_logits, hg_norm, hg_w_out):
    lb_p = jax.nn.softmax(hg_lb_logits.astype(jnp.float32), axis=0)
    lower_bounds = jnp.cumsum(lb_p, axis=0) - lb_p[0]
    silu_c = jax.nn.silu(c)[:, None, :]
    silu_cc = jax.nn.silu(c_ctx)[None, None, :]
    xl, xc = x, ctx
    for i in range(DEPTH):
        last = i == DEPTH - 1
        mod_l = jnp.split(silu_c @ ada_w[i] + ada_b[i], 6, axis=-1)
        mod_c = jnp.split(silu_cc @ ada_w[i] + ada_b[i], 6, axis=-1)
        col = i % 2 == 1
        hl = to_scan_order(modulate(rmsnorm(xl, norm_mix[i]), mod_l[0], mod_l[1]), col)
        hc = modulate(rmsnorm(xc, norm_mix[i]), mod_c[0], mod_c[1])
        kind, j = i % N_MIXERS, i // N_MIXERS
        if kind == 0:
            oc, ol = gdn_mixer(hc, hl, gdn_w_in[j], gdn_conv[j], gdn_a_log[j], gdn_dt_bias[j],
                               gdn_norm[j], gdn_w_out[j], not last)
        elif kind == 1:
            oc, ol = rglru_mixer(hc, hl, lru_w_in[j], lru_conv_w[j], lru_conv_b[j], lru_w_r[j],
                                 lru_b_r[j], lru_w_i[j], lru_b_i[j], lru_lambda[j], lru_w_out[j],
                                 not last)
        else:
            oc, ol = hgrn2_mixer(hc, hl, hg_w_in[j], lower_bounds[i], hg_norm[j], hg_w_out[j],
                                 not last)
        xl = xl + mod_l[2] * from_scan_order(ol, col)
        xl = xl + mod_l[5] * swiglu(modulate(rmsnorm(xl, norm_ffn[i]), mod_l[3], mod_l[4]),
                                    ffn_w1[i], ffn_w3[i], ffn_w2[i])
        if not last:
            xc = xc + mod_c[2] * oc
            xc = xc + mod_c[5] * swiglu(modulate(rmsnorm(xc, norm_ffn[i]), mod_c[3], mod_c[4]),
                                        ffn_w1[i], ffn_w3[i], ffn_w2[i])
    return rmsnorm(xl, norm_final)
```

```python
from contextlib import ExitStack
import numpy as np
import concourse.bass as bass
import concourse.mybir as mybir
from concourse.bass_utils import run_bass_kernel_spmd

F32 = mybir.dt.float32
BF16 = mybir.dt.bfloat16
AF = mybir.ActivationFunctionType
ALU = mybir.AluOpType
AX = mybir.AxisListType
ENGS = ("pe", "act", "dve", "pool", "sp")
EPS = 1e-6
NCORE = 8


def _k(ap):
    if ap is None or isinstance(ap, (int, float)):
        return None
    return (ap.name, None)


def _norm(k):
    if k is None:
        return None
    if isinstance(k, str):
        return (k, None)
    return k


class Prog:
    EPOCH = 20000
    NDMA = 24

    def __init__(self):
        self.nc = bass.Bass("TRN2", target_bir_lowering=False)
        self.stack = ExitStack()
        self.ops = {e: [] for e in ENGS}
        self.seq = {e: 0 for e in ENGS}
        self.track = {}
        self.waited = {e: {} for e in ENGS}
        self.dma_k = 0
        self.out_tokens = []
        self.cnt = 0
        self.needed = {e: set() for e in ENGS}

    def sb(self, name, shape, dt=F32):
        return self.stack.enter_context(self.nc.sbuf_tensor(name, list(shape), dt))

    def ps(self, name, shape, dt=F32):
        return self.stack.enter_context(self.nc.psum_tensor(name, list(shape), dt))

    def dram(self, name, shape, dt=F32, kind="ExternalInput"):
        return self.nc.dram_tensor(name, list(shape), dt, kind=kind).ap()

    def _states(self, key):
        name, sub = key
        d = self.track.get(name)
        if d is None:
            return []
        if sub is None:
            return list(d.values())
        out = []
        if None in d:
            out.append(d[None])
        if sub in d:
            out.append(d[sub])
        return out

    def _emit(self, e, fn, reads, writes, dma=False):
        reads = [_norm(k) for k in reads if k is not None]
        writes = [_norm(k) for k in writes if k is not None]
        deps = {}

        def add(d):
            for sk, v in d.items():
                if e == "pe" and sk == ("eng", "pe") and not dma:
                    continue
                if deps.get(sk, 0) < v:
                    deps[sk] = v

        for k in reads:
            for st in self._states(k):
                add(st["w"])
        for k in writes:
            for st in self._states(k):
                add(st["w"])
                add(st["r"])
        if dma:
            slot = self.dma_k % self.NDMA
            cnt = self.dma_k // self.NDMA + 1
            self.dma_k += 1
            if cnt > 1:
                sk = ("dma", slot)
                deps[sk] = max(deps.get(sk, 0), cnt - 1)
            sk, v = ("dma", slot), cnt
        else:
            self.seq[e] += 1
            sk, v = ("eng", e), self.seq[e]
        waits = []
        wd = self.waited[e]
        for dk, dv in deps.items():
            if wd.get(dk, 0) < dv:
                wd[dk] = dv
                waits.append((dk, dv))
                if dk[0] == "eng":
                    self.needed[dk[1]].add(dv)
        self.ops[e].append((fn, waits, sk, v))
        self.cnt += 1
        for name, sub in reads:
            d = self.track.setdefault(name, {})
            st = d.setdefault(sub, {"w": {}, "r": {}})
            if st["r"].get(sk, 0) < v:
                st["r"][sk] = v
        for name, sub in writes:
            d = self.track.setdefault(name, {})
            if sub is None:
                d.clear()
            d[sub] = {"w": {sk: v}, "r": {}}
        return (sk, v)

    def _keys(self, aps, override):
        if override is not None:
            return list(override)
        return [_k(a) for a in aps]

    def mm(self, out, lhsT, rhs, start=True, stop=True, rk=None, wk=None):
        return self._emit("pe", lambda eng: eng.matmul(out, lhsT, rhs, start=start, stop=stop),
                          self._keys([lhsT, rhs], rk), self._keys([out], wk))

    def transpose(self, out, in_, ident, rk=None, wk=None):
        return self._emit("pe", lambda eng: eng.transpose(out, in_, ident),
                          self._keys([in_, ident], rk), self._keys([out], wk))

    def act(self, out, in_, func, bias=0.0, scale=1.0, rk=None, wk=None):
        return self._emit("act", lambda e: e.activation(out, in_, func, bias=bias, scale=scale),
                          self._keys([in_, bias, scale], rk), self._keys([out], wk))

    def v(self, eng, meth, out, ins, *args, rk=None, wk=None, **kw):
        rextra = [a for a in list(args) + list(kw.values()) if hasattr(a, "name") and hasattr(a, "shape")]
        return self._emit(eng, lambda e: getattr(e, meth)(out, *ins, *args, **kw),
                          self._keys(list(ins) + rextra, rk), self._keys([out], wk))

    def dma(self, out, in_, eng="sp", rk=None, wk=None, is_output=False):
        tok = self._emit(eng, lambda e: e.dma_start(out=out, in_=in_),
                         self._keys([in_], rk), self._keys([out], wk), dma=True)
        if is_output:
            self.out_tokens.append(tok)
        return tok

    def finish(self):
        nc = self.nc
        E = self.EPOCH
        rank = {e: {sq: i for i, sq in enumerate(sorted(self.needed[e]))} for e in ENGS}

        def semval(sk, v):
            if sk[0] == "dma":
                return sk, 16 * v
            r = rank[sk[1]][v]
            return ("eng", sk[1], r // E), r % E + 1

        fin = {}
        for sk, v in self.out_tokens:
            fin[sk] = max(fin.get(sk, 0), v)
        plan = {e: [] for e in ENGS}
        semkeys = set()
        for e in ENGS:
            for fn, waits, sk, v in self.ops[e]:
                w2 = [semval(a, b) for a, b in waits]
                if sk[0] == "dma":
                    inc = (sk, 16)
                elif v in rank[e]:
                    inc = (semval(sk, v)[0], 1)
                else:
                    inc = None
                plan[e].append((fn, w2, inc))
                for a, _ in w2:
                    semkeys.add(a)
                if inc:
                    semkeys.add(inc[0])
        finw = [semval(sk, v) for sk, v in fin.items()]
        for a, _ in finw:
            semkeys.add(a)
        sems = {}
        for sk in sorted(semkeys, key=str):
            sems[sk] = self.stack.enter_context(nc.semaphore("s_" + "_".join(str(x) for x in sk)))
        self.n_inc = sum(1 for e in ENGS for p in plan[e] if p[2])

        def run(e, eng):
            for fn, waits, inc in plan[e]:
                for w, v in waits:
                    eng.wait_ge(sems[w], v)
                ins = fn(eng)
                if inc:
                    ins.then_inc(sems[inc[0]], inc[1])
            if e == "sp":
                for w, v in finw:
                    eng.wait_ge(sems[w], v)

        with nc.Block() as block:
            @block.tensor
            def _(eng):
                run("pe", eng)

            @block.scalar
            def _(eng):
                run("act", eng)

            @block.vector
            def _(eng):
                run("dve", eng)

            @block.gpsimd
            def _(eng):
                run("pool", eng)

            @block.sync
            def _(eng):
                run("sp", eng)
        self.stack.close()
        return nc


class Rot:
    def __init__(self, items):
        self.items = items
        self.i = 0

    def get(self):
        x = self.items[self.i % len(self.items)]
        self.i += 1
        return x


def psum_pool(P, n, name="ps", w=512):
    return Rot([P.ps(f"{name}{i}", [128, w]) for i in range(n)])


def launch(P, in_maps):
    nc = P.finish()
    res = run_bass_kernel_spmd(nc, in_maps, core_ids=list(range(NCORE)))
    return res.results


def load_w_bf16(P, name, dram_ap, kc, cout, split=1):
    w = P.sb(name, [128, kc, cout], BF16)
    v = dram_ap.rearrange("(k p) c -> p k c", p=128)
    for k in range(kc):
        P.dma(w[:, k, :], v[:, k, :], eng="pool", wk=[(name, k)])
    return w


def make_const(P, name, val, shape=(128, 128), dt=BF16):
    t = P.sb(name, list(shape), dt)
    P.v("dve", "memset", t[:], [], val)
    return t


def norm_mod(P, x, h, nt, A, B, ones_bf, sq, psp, rstd, tmp, nk=8, dnorm=1024.0):
    P.act(sq[:, :, :nt], x[:, :, :nt], AF.Square)
    ps = psp.get()
    for k in range(nk):
        P.mm(ps[:, :nt], ones_bf[:], sq[:, k, :nt], start=(k == 0), stop=(k == nk - 1))
    P.act(rstd[:, :nt], ps[:, :nt], AF.Sqrt, scale=1.0 / dnorm, bias=EPS)
    P.v("dve", "reciprocal", rstd[:, :nt], [rstd[:, :nt]])
    tn, hn = tmp.name, h.name
    for k in range(nk):
        P.v("dve", "tensor_tensor", tmp[:, k, :nt], [x[:, k, :nt], rstd[:, :nt]], ALU.mult, wk=[(tn, k)])
        if B is None:
            P.v("dve", "tensor_scalar", h[:, k, :nt], [tmp[:, k, :nt]], A[:, k:k + 1], None, ALU.mult,
                rk=[(tn, k), _k(A)], wk=[(hn, k)])
        else:
            P.act(h[:, k, :nt], tmp[:, k, :nt], AF.Identity, scale=A[:, k:k + 1], bias=B[:, k:k + 1],
                  rk=[(tn, k), _k(A), _k(B)], wk=[(hn, k)])


def build_k1():
    P = Prog()
    cT = P.dram("cT", [128, 8 * 5])
    w = P.dram("w", [4 * 1024, 768])
    b = P.dram("b", [128, 24])
    out = P.dram("mod", [24 * 128, 5], kind="ExternalOutput")
    c_sb = P.sb("c_sb", [128, 40])
    sc = P.sb("sc", [128, 40])
    b_sb = P.sb("b_sb", [128, 24])
    P.dma(c_sb[:], cT)
    P.dma(b_sb[:], b)
    P.act(sc[:], c_sb[:], AF.Silu)
    wv = w.rearrange("(l k p) c -> l p k c", p=128, k=8)
    wsb = [P.sb(f"w{l}", [128, 8, 768]) for l in range(4)]
    for l in range(4):
        for k in range(8):
            P.dma(wsb[l][:, k, :], wv[l][:, k, :], wk=[(f"w{l}", k)])
    psp = psum_pool(P, 4, w=16)
    res = P.sb("res", [128, 24, 5])
    for l in range(4):
        for m in range(6):
            ps = psp.get()
            for k in range(8):
                P.mm(ps[:, :5], wsb[l][:, k, m * 128:(m + 1) * 128], sc[:, k * 5:(k + 1) * 5],
                     start=(k == 0), stop=(k == 7), rk=[(f"w{l}", k), _k(sc)])
            j = l * 6 + m
            P.v("dve", "tensor_scalar", res[:, j, :], [ps[:, :5]], b_sb[:, j:j + 1], None, ALU.add,
                wk=[("res", j)])
    P.dma(out.rearrange("(j p) n -> p j n", p=128), res[:], is_output=True)
    return P


def run_k1(inp):
    c5 = np.concatenate([inp["c"], inp["c_ctx"][None]], 0).astype(np.float32)
    cT = np.ascontiguousarray(c5.T.reshape(8, 128, 5).transpose(1, 0, 2)).reshape(128, 40)
    maps = []
    for c in range(NCORE):
        w = np.ascontiguousarray(inp["ada_w"][:, :, c * 768:(c + 1) * 768]).reshape(4 * 1024, 768)
        bb = inp["ada_b"][:, c * 768:(c + 1) * 768].reshape(4, 6, 128).transpose(2, 0, 1).reshape(128, 24)
        maps.append({"cT": cT, "w": w, "b": np.ascontiguousarray(bb)})
    res = launch(build_k1(), maps)
    mod = np.zeros((4, 6144, 5), np.float32)
    for c in range(NCORE):
        r = res[c]["mod"].reshape(4, 6, 128, 5)
        mod[:, c * 768:(c + 1) * 768, :] = r.reshape(4, 768, 5)
    return mod


def mod_table(mod, l, col):
    return np.ascontiguousarray(mod[l, :, col].reshape(48, 128).T)


def vec_table(v):
    v = np.asarray(v, np.float32)
    return np.ascontiguousarray(v.reshape(-1, 128).T)


FFN_H = 2816


def build_k4b(ntiles_ctx, ntiles_lat, NT=256, final=False):
    P = Prog()
    ntok = (ntiles_ctx + ntiles_lat) * NT
    xT = P.dram("xT", [1024, ntok])
    w1 = P.dram("w1", [1024, FFN_H]); w3 = P.dram("w3", [1024, FFN_H]); w2 = P.dram("w2", [FFN_H, 1024])
    tabs = P.dram("tabs", [128, 2 * 48 + 16])
    outT = P.dram("outT", [1024, ntok], kind="ExternalOutput")
    tb = P.sb("tb", [128, 112])
    P.dma(tb[:], tabs)
    ones = make_const(P, "ones", 1.0)
    AB = P.sb("AB", [128, 2, 8])
    for s in range(2):
        base = s * 48
        P.v("dve", "scalar_tensor_tensor", AB[:, s, :], [tb[:, base + 32:base + 40]], 1.0, tb[:, 96:104], ALU.add, ALU.mult,
            wk=[("AB", s)])
    W1 = load_w_bf16(P, "W1", w1, 8, FFN_H)
    W3 = load_w_bf16(P, "W3", w3, 8, FFN_H)
    W2 = load_w_bf16(P, "W2", w2, 22, 1024)
    xs = Rot([P.sb(f"x{i}", [128, 8, NT]) for i in range(2)])
    xo = P.sb("xo", [128, 8, NT])
    sq = P.sb("sq", [128, 8, NT], BF16)
    h = P.sb("h", [128, 8, NT], BF16)
    g = P.sb("g", [128, 22, NT], BF16)
    rstd = P.sb("rstd", [128, NT])
    sas = Rot([P.sb(f"sa{i}", [128, NT]) for i in range(2)])
    psp = psum_pool(P, 6, w=NT)
    xv = xT.rearrange("(k p) t -> p k t", p=128)
    ov = outT.rearrange("(k p) t -> p k t", p=128)
    nt = NT
    tiles = [(i, 1) for i in range(ntiles_ctx)] + [(ntiles_ctx + i, 0) for i in range(ntiles_lat)]
    xcur = xs.get()
    P.dma(xcur[:], xv[:, :, 0:NT])
    for ti, (t, s) in enumerate(tiles):
        x = xcur
        if ti + 1 < len(tiles):
            xcur = xs.get()
            t2 = tiles[ti + 1][0]
            P.dma(xcur[:], xv[:, :, t2 * NT:(t2 + 1) * NT])
        base = s * 48
        norm_mod(P, x, h, nt, AB[:, s, :], tb[:, base + 24:base + 32], ones, sq, psp, rstd, xo)
        for m in range(22):
            pa = psp.get(); pb = psp.get()
            for k in range(8):
                P.mm(pa[:, :nt], W1[:, k, m * 128:(m + 1) * 128], h[:, k, :], start=(k == 0), stop=(k == 7),
                     rk=[("W1", k), ("h", k)])
            for k in range(8):
                P.mm(pb[:, :nt], W3[:, k, m * 128:(m + 1) * 128], h[:, k, :], start=(k == 0), stop=(k == 7),
                     rk=[("W3", k), ("h", k)])
            sa = sas.get()
            P.act(sa[:], pa[:, :nt], AF.Silu)
            P.v("dve", "tensor_tensor", g[:, m, :], [sa[:], pb[:, :nt]], ALU.mult, wk=[("g", m)])
        for mo in range(8):
            py = psp.get()
            for k in range(22):
                P.mm(py[:, :nt], W2[:, k, mo * 128:(mo + 1) * 128], g[:, k, :], start=(k == 0), stop=(k == 21),
                     rk=[("W2", k), ("g", k)])
            P.v("dve", "scalar_tensor_tensor", xo[:, mo, :], [py[:, :nt]], tb[:, base + 40 + mo:base + 41 + mo], x[:, mo, :],
                ALU.mult, ALU.add, wk=[("xo", mo)])
        if final:
            norm_mod(P, xo, xo, nt, tb[:, 104:112], None, ones, sq, psp, rstd, x)
        P.dma(ov[:, :, t * NT:(t + 1) * NT], xo[:], is_output=True)
    return P


def tok_split(XL, XC, c):
    b, hf = c // 2, c % 2
    return XC[b, hf * 128:(hf + 1) * 128], XL[b, hf * 4096:(hf + 1) * 4096]


def pack_tok(xc, xl, with_ctx=True, cpad=256):
    C = xl.shape[1]
    if not with_ctx:
        return np.ascontiguousarray(xl.T)
    out = np.zeros((C, cpad + xl.shape[0]), np.float32)
    out[:, :xc.shape[0]] = xc.T
    out[:, cpad:] = xl.T
    return out


def unpack_tok(oT, XL, XC, c, with_ctx=True, cpad=256):
    b, hf = c // 2, c % 2
    if with_ctx:
        XC[b, hf * 128:(hf + 1) * 128] = oT[:, :128].T
        XL[b, hf * 4096:(hf + 1) * 4096] = oT[:, cpad:].T
    else:
        XL[b, hf * 4096:(hf + 1) * 4096] = oT.T


def run_ffn(XL, XC, mod, l, inp, final):
    with_ctx = not final
    P = build_k4b(1 if with_ctx else 0, 16, NT=256, final=final)
    maps = []
    for c in range(NCORE):
        xc, xl = tok_split(XL, XC, c)
        tabs = np.concatenate([mod_table(mod, l, c // 2), mod_table(mod, l, 4),
                               vec_table(inp["norm_ffn"][l]), vec_table(inp["norm_final"])], 1)
        maps.append({"xT": pack_tok(xc, xl, with_ctx), "w1": inp["ffn_w1"][l], "w3": inp["ffn_w3"][l],
                     "w2": inp["ffn_w2"][l], "tabs": np.ascontiguousarray(tabs)})
    res = launch(P, maps)
    XL2, XC2 = np.empty_like(XL), XC.copy()
    for c in range(NCORE):
        unpack_tok(res[c]["outT"], XL2, XC2, c, with_ctx)
    return XL2, XC2


def k2_tiles(conv):
    tiles = []
    if conv:
        louts = [509] * 8 + [24]
        tiles.append((0, 131, 0, 128, 1, 0, 1))
        in_off, out_off = 131, 128
        for i, n in enumerate(louts):
            tiles.append((in_off, n + 3, out_off, n, 0, 2 if i == 0 else None, 3 if i == len(louts) - 1 else None))
            in_off += n + 3
            out_off += n
    else:
        tiles.append((0, 128, 0, 128, 1, None, None))
        for i in range(8):
            tiles.append((128 + i * 512, 512, 128 + i * 512, 512, 0, None, None))
    return tiles


K2_COUT = {"gdn": 4128, "lru": 2048, "hg": 5120}
K2_ROWS = {"gdn": 4160, "lru": 2048, "hg": 7 * 1024}


def build_k2(kind, hg_layer=2):
    P = Prog()
    conv = kind in ("gdn", "lru")
    tiles = k2_tiles(conv)
    ncols_total = sum(t[1] for t in tiles)
    cout, rows_out = K2_COUT[kind], K2_ROWS[kind]
    xT = P.dram("xT", [1024, ncols_total])
    w = P.dram("w", [1024, cout])
    tabs = P.dram("tabs", [128, 104])
    hmask = P.dram("hmask", [128, 4])
    outT = P.dram("outT", [rows_out, 4224], kind="ExternalOutput")
    tb = P.sb("tb", [128, 104]); P.dma(tb[:], tabs)
    hm = P.sb("hm", [128, 4]); P.dma(hm[:], hmask)
    ones = make_const(P, "ones", 1.0)
    AB = P.sb("AB", [128, 2, 8])
    for s in range(2):
        P.v("dve", "scalar_tensor_tensor", AB[:, s, :], [tb[:, s * 48 + 8:s * 48 + 16]], 1.0, tb[:, 96:104],
            ALU.add, ALU.mult, wk=[("AB", s)])
    nconv = {"gdn": 24, "lru": 8, "hg": 0}[kind]
    if conv:
        cw_d = P.dram("convw", [128, 4 * nconv])
        cw = P.sb("cw", [128, 4 * nconv]); P.dma(cw[:], cw_d)
        identf = P.sb("identf", [128, 128])
        id_d = P.dram("ident", [128, 128]); P.dma(identf[:], id_d)
        dg = P.sb("dg", [128, 4 * nconv, 128], BF16)
        for i in range(4 * nconv):
            P.v("dve", "tensor_scalar", dg[:, i, :], [identf[:]], cw[:, i:i + 1], None, ALU.mult, wk=[("dg", i)])
    if kind == "gdn":
        ab_d = P.dram("abtab", [32, 2])
        abt = P.sb("abt", [32, 2]); P.dma(abt[:], ab_d)
        nexpA = P.sb("nexpA", [32, 1])
        P.act(nexpA[:], abt[:, 1:2], AF.Exp)
        P.v("dve", "tensor_scalar", nexpA[:], [nexpA[:]], -1.0, None, ALU.mult)
    if kind == "lru":
        cb_d = P.dram("convb", [128, 8])
        cb = P.sb("cb", [128, 8]); P.dma(cb[:], cb_d)
    if kind == "hg":
        lb_d = P.dram("lbl_in", [128, 32])
        lbl = P.sb("lbl", [128, 8, 4]); P.dma(lbl[:], lb_d.rearrange("p (c l) -> p c l", l=4))
        el = P.sb("el", [128, 8, 4])
        P.act(el[:], lbl[:], AF.Exp)
        den = P.sb("den", [128, 8]); num = P.sb("num", [128, 8])
        P.v("dve", "tensor_reduce", den[:], [el[:]], AX.X, ALU.add)
        P.v("dve", "tensor_reduce", num[:], [el[:, :, 1:hg_layer + 1]], AX.X, ALU.add)
        lb = P.sb("lb", [128, 8]); oml = P.sb("oml", [128, 8])
        P.v("dve", "reciprocal", den[:], [den[:]])
        P.v("dve", "tensor_tensor", lb[:], [num[:], den[:]], ALU.mult)
        P.v("dve", "tensor_scalar", oml[:], [lb[:]], -1.0, 1.0, ALU.mult, ALU.add)
    W = load_w_bf16(P, "W", w, 8, cout)
    xs = Rot([P.sb(f"x{i}", [128, 8, 512]) for i in range(2)])
    tmp = P.sb("tmp", [128, 8, 512])
    sq = P.sb("sq", [128, 8, 512], BF16)
    h = P.sb("h", [128, 8, 512], BF16)
    rstd = P.sb("rstd", [128, 512])
    psp = psum_pool(P, 5)
    psc = psum_pool(P, 2, "pc")
    pbs = Rot([P.sb(f"pb{i}", [128, 512], BF16) for i in range(3)])
    obs = Rot([P.sb(f"ob{i}", [128, 512]) for i in range(6)])
    sts = Rot([P.sb(f"st{i}", [128, 512]) for i in range(3)])
    sqs = Rot([P.sb(f"sqh{i}", [128, 512], BF16) for i in range(2)])
    rss = Rot([P.sb(f"rs{i}", [128, 512]) for i in range(2)])
    xv = xT.rearrange("(k p) t -> p k t", p=128)

    def store(row0, rows, off, n, ob):
        P.dma(outT[row0:row0 + rows, off:off + n], ob[:rows, :n], is_output=True)

    xcur = xs.get()
    P.dma(xcur[:, :, :tiles[0][1]], xv[:, :, 0:tiles[0][1]])
    for ti, (ioff, nc_, ooff, n, isctx, hl, hr) in enumerate(tiles):
        x = xcur
        if ti + 1 < len(tiles):
            xcur = xs.get()
            o2, n2 = tiles[ti + 1][0], tiles[ti + 1][1]
            P.dma(xcur[:, :, :n2], xv[:, :, o2:o2 + n2])
        s = isctx
        norm_mod(P, x, h, nc_, AB[:, s, :], tb[:, s * 48:s * 48 + 8], ones, sq, psp, rstd, tmp)
        nchunks = (cout + 127) // 128
        for m in range(nchunks):
            rows = min(128, cout - m * 128)
            ps = psp.get()
            for k in range(8):
                P.mm(ps[:rows, :nc_], W[:, k, m * 128:m * 128 + rows], h[:, k, :nc_], start=(k == 0), stop=(k == 7),
                     rk=[("W", k), ("h", k)])
            if m < nconv and not (kind == "lru") or (kind == "lru" and m >= 8):
                ci = m if kind == "gdn" else m - 8
                pb = pbs.get()
                P.act(pb[:, :nc_], ps[:, :nc_], AF.Copy)
                if hl is not None:
                    P.v("dve", "tensor_scalar", pb[:, 0:2], [pb[:, 0:2]], hm[:, hl:hl + 1], None, ALU.mult)
                if hr is not None:
                    P.v("dve", "tensor_scalar", pb[:, nc_ - 1:nc_], [pb[:, nc_ - 1:nc_]], hm[:, hr:hr + 1], None, ALU.mult)
                pc = psc.get()
                for j in range(4):
                    P.mm(pc[:, :n], dg[:, j * nconv + ci, :], pb[:, j:j + n], start=(j == 0), stop=(j == 3),
                         rk=[("dg", j * nconv + ci), _k(pb)])
                ob = obs.get()
                if kind == "lru":
                    P.act(ob[:, :n], pc[:, :n], AF.Identity, bias=cb[:, ci:ci + 1])
                    store(m * 128, 128, ooff, n, ob)
                elif m >= 16:
                    P.act(ob[:, :n], pc[:, :n], AF.Silu)
                    store(m * 128, 128, ooff, n, ob)
                else:
                    st = sts.get(); sqh = sqs.get(); rs = rss.get()
                    P.act(st[:, :n], pc[:, :n], AF.Silu)
                    P.v("pool", "tensor_tensor", sqh[:, :n], [st[:, :n], st[:, :n]], ALU.mult)
                    pss = psc.get()
                    P.mm(pss[:, :n], ones[:], sqh[:, :n])
                    P.act(rs[:, :n], pss[:, :n], AF.Sqrt, bias=EPS)
                    P.v("dve", "reciprocal", rs[:, :n], [rs[:, :n]])
                    P.v("dve", "scalar_tensor_tensor", ob[:, :n], [st[:, :n]], (128.0 ** -0.5) if m < 8 else 1.0, rs[:, :n],
                        ALU.mult, ALU.mult)
                    store(m * 128, 128, ooff, n, ob)
                continue
            c0 = 1 if conv else 0
            lo = 2 if conv else 0
            src = ps[:rows, lo:lo + n]
            if kind == "gdn" and m < 32:
                ob = obs.get()
                P.v("dve", "tensor_copy", ob[:, :n], [src])
                store(m * 128, 128, ooff, n, ob)
            elif kind == "gdn":
                ob = obs.get(); ob2 = obs.get()
                P.act(ob[:32, :n], src, AF.Exp, bias=abt[:, 0:1])
                P.act(ob[:32, :n], ob[:32, :n], AF.Ln, bias=1.0)
                P.v("dve", "tensor_scalar", ob[:32, :n], [ob[:32, :n]], nexpA[:, 0:1], None, ALU.mult)
                store(4096, 32, ooff, n, ob)
                P.act(ob2[:32, :n], src, AF.Sigmoid)
                store(4128, 32, ooff, n, ob2)
            elif kind == "lru":
                ob = obs.get()
                P.act(ob[:, :n], src, AF.Gelu)
                store(m * 128, 128, ooff, n, ob)
            else:
                ob = obs.get()
                if m < 8:
                    P.act(ob[:, :n], src, AF.Silu)
                    store(m * 128, 128, ooff, n, ob)
                elif m < 24:
                    d = (m - 8) // 8; c = (m - 8) % 8
                    ob2 = obs.get()
                    P.act(ob[:, :n], src, AF.Sigmoid)
                    P.v("dve", "tensor_scalar", ob[:, :n], [ob[:, :n]], oml[:, c:c + 1], lb[:, c:c + 1], ALU.mult, ALU.add)
                    P.act(ob2[:, :n], ob[:, :n], AF.Ln)
                    store(1024 + d * 2048 + 1024 + c * 128, 128, ooff, n, ob2)
                    ob3 = obs.get()
                    P.v("dve", "tensor_scalar", ob3[:, :n], [ob[:, :n]], -1.0, 1.0, ALU.mult, ALU.add)
                    store(1024 + d * 2048 + c * 128, 128, ooff, n, ob3)
                else:
                    P.v("dve", "tensor_copy", ob[:, :n], [src])
                    store(5120 + (m - 24) * 128, 128, ooff, n, ob)
    return P


T_ALL = 8448
NCHUNK = T_ALL // 64
GRP = 4
GDN_NB = 2


def chunk_consts():
    i = np.arange(64)
    U = (i[:, None] <= i[None, :]).astype(np.float32)
    SU = (i[:, None] > i[None, :]).astype(np.float32)
    NEG = np.where(i[:, None] >= i[None, :], 0.0, -30000.0).astype(np.float32)
    POS = np.where(i[None, :] >= i[:, None], 0.0, 30000.0).astype(np.float32)
    SM = (i[:, None] > i[None, :]).astype(np.float32)
    IM = (i[None, :] >= i[:, None]).astype(np.float32)
    ID = np.eye(64, dtype=np.float32)
    ON = np.ones((64, 64), np.float32)
    return np.ascontiguousarray(np.concatenate([U, SU, NEG, POS, SM, IM, ID, ON], 1))


def load_consts(P):
    cd = P.dram("cst", [64, 512])
    c = P.sb("cst_sb", [64, 512]); P.dma(c[:], cd)
    names = ["U", "SU", "NEG", "POS", "SM", "IM", "ID", "ON"]
    C = {n: c[:, i * 64:(i + 1) * 64] for i, n in enumerate(names)}
    on128 = P.sb("on128", [64, 128]); P.v("dve", "memset", on128[:], [], 1.0)
    C["ON128"] = on128
    return C


def gdn_consts():
    i = np.arange(64)
    U = (i[:, None] <= i[None, :]).astype(np.float32)
    SU = (i[:, None] > i[None, :]).astype(np.float32)
    ON = np.ones((64, 64), np.float32)
    NEG = np.where(i[:, None] >= i[None, :], 0.0, -30000.0).astype(np.float32)
    NEGT = np.where(i[None, :] >= i[:, None], 0.0, -30000.0).astype(np.float32)
    ID = np.eye(64, dtype=np.float32)
    SM = (i[:, None] > i[None, :]).astype(np.float32)
    rep = lambda m: np.tile(m, (1, 8))
    return np.ascontiguousarray(np.concatenate([U, -U, ON, -ON, NEG, NEGT, ID, SU, rep(U), rep(ID), rep(SM), rep(ON)], 1))


def build_k3_gdn(nchunk=NCHUNK, dbg=0):
    P = Prog()
    T = nchunk * 64
    G2 = 2
    qT = P.dram("qT", [1024, T]); kT = P.dram("kT", [1024, T])
    k_tm = P.dram("k_tm", [T, 1024]); v_tm = P.dram("v_tm", [T, 1024])
    g_tm = P.dram("g_tm", [T, 8]); b_tm = P.dram("b_tm", [T, 8])
    o_tm = P.dram("o_tm", [T, 1024], kind="ExternalOutput")
    cd = P.dram("cst", [64, 2560])
    c = P.sb("cst_sb", [64, 2560]); P.dma(c[:], cd)
    cn = ["U", "NU", "ON", "NON", "NEG", "NEGT", "ID", "SU"]
    C = {n: c[:, i * 64:(i + 1) * 64] for i, n in enumerate(cn)}
    for i, n in enumerate(["U3", "ID3", "SM3", "ON3"]):
        C[n] = c[:, 512 + i * 512:512 + (i + 1) * 512].rearrange("p (h j) -> p h j", h=8)
    on128 = P.sb("on128", [64, 128]); P.v("dve", "memset", on128[:], [], 1.0)
    S = P.sb("S", [128, 8, 128]); P.v("dve", "memset", S[:], [], 0.0)
    qTv = qT.rearrange("(h p) t -> p h t", p=128); kTv = kT.rearrange("(h p) t -> p h t", p=128)
    kv = k_tm.rearrange("(c p) f -> p c f", p=64); vv = v_tm.rearrange("(c p) f -> p c f", p=64)
    gv = g_tm.rearrange("(c p) f -> p c f", p=64); bv = b_tm.rearrange("(c p) f -> p c f", p=64)
    ov = o_tm.rearrange("(c p) f -> p c f", p=64)
    NB = 2
    qTs = [P.sb(f"qTs{i}", [128, 8, G2 * 64]) for i in range(NB)]
    kTs = [P.sb(f"kTs{i}", [128, 8, G2 * 64]) for i in range(NB)]
    ks = [P.sb(f"ks{i}", [64, G2, 1024]) for i in range(NB)]
    vs = [P.sb(f"vs{i}", [64, G2, 1024]) for i in range(NB)]
    gs = [P.sb(f"gs{i}", [64, G2, 8]) for i in range(NB)]
    bs = [P.sb(f"bs{i}", [64, G2, 8]) for i in range(NB)]
    os_ = [P.sb(f"os{i}", [64, G2, 1024]) for i in range(NB)]
    pss = psum_pool(P, 4, "pss")
    psb = Rot([P.ps(f"psb{i}", [128, 1024]) for i in range(2)])

    def R(name, shape, n):
        return Rot([P.sb(f"{name}{i}", shape) for i in range(n)])

    def bc(ap, n):
        return ap.unsqueeze(2).to_broadcast([ap.shape[0], 8, n])

    gc_r, a_r, kdec_r, ba_r, nb_r = [R(n, [64, 8], 2) for n in ("gc", "a", "kdec", "ba", "nb")]
    dl_r = R("dl", [128, 8], 2)
    G3_r, UG3_r, t1_r, nbSM_r, dA_r = [R(n, [64, 8, 64], 1) for n in ("G3", "UG3", "t1", "nbSM", "dA")]
    D3_r, DT3_r = R("D3", [64, 8, 64], 2), R("DT3", [64, 8, 64], 2)
    X_r, XT_r, TT_r = R("X", [64, 8, 64], 3), R("XT", [64, 8, 64], 3), R("TT", [64, 8, 64], 3)
    aqk_r = R("aqk", [128, 8, 64], 2)
    vb_r, kba_r, kd_r, u_r = [R(n, [64, 8, 128], 2) for n in ("vb", "kba", "kd", "u")]
    wT_r, qd_r = R("wT", [128, 8, 64], 2), R("qd", [128, 8, 64], 2)
    vn_r = R("vn", [128, 8, 128], 2); sd_r = R("sd", [128, 8, 128], 1)
    for r_ in (aqk_r, vn_r):
        for t_ in r_.items:
            P.v("dve", "memset", t_[:], [], 0.0)
    ngrp = (nchunk + G2 - 1) // G2

    def load_group(gi):
        b = gi % NB
        c0 = gi * G2; n = min(G2, nchunk - c0)
        P.dma(qTs[b][:, :, :n * 64], qTv[:, :, c0 * 64:(c0 + n) * 64])
        P.dma(kTs[b][:, :, :n * 64], kTv[:, :, c0 * 64:(c0 + n) * 64])
        P.dma(ks[b][:, :n, :], kv[:, c0:c0 + n, :])
        P.dma(vs[b][:, :n, :], vv[:, c0:c0 + n, :])
        P.dma(gs[b][:, :n, :], gv[:, c0:c0 + n, :])
        P.dma(bs[b][:, :n, :], bv[:, c0:c0 + n, :])

    H = range(8)
    load_group(0)
    for gi in range(ngrp):
        if gi + 1 < ngrp:
            load_group(gi + 1)
        b = gi % NB
        c0 = gi * G2; n = min(G2, nchunk - c0)
        for ci in range(n):
            g_c = gs[b][:, ci, :]; be_c = bs[b][:, ci, :]
            kTc = kTs[b][:, :, ci * 64:(ci + 1) * 64]; qTc = qTs[b][:, :, ci * 64:(ci + 1) * 64]
            k_c = ks[b][:, ci, :].rearrange("p (h d) -> p h d", h=8); v_c = vs[b][:, ci, :].rearrange("p (h d) -> p h d", h=8)
            pg = pss.get()
            P.mm(pg[:64, 0:8], C["U"], g_c)
            P.mm(pg[:64, 8:16], C["SU"], g_c)
            P.mm(pg[:, 16:24], on128[:], g_c)
            a = a_r.get(); kdec = kdec_r.get(); ba = ba_r.get(); nb = nb_r.get(); dl = dl_r.get()
            P.act(a[:], pg[:64, 0:8], AF.Exp)
            P.act(kdec[:], pg[:64, 8:16], AF.Exp)
            P.act(dl[:], pg[:, 16:24], AF.Exp)
            P.v("dve", "tensor_tensor", ba[:], [be_c, a[:]], ALU.mult)
            P.v("dve", "tensor_scalar", nb[:], [be_c], -1.0, None, ALU.mult)
            G3 = G3_r.get(); UG3 = UG3_r.get()
            P.v("dve", "tensor_tensor", G3[:], [C["ON3"], bc(g_c, 64)], ALU.mult)
            P.v("dve", "tensor_tensor", UG3[:], [C["U3"], bc(g_c, 64)], ALU.mult)
            if dbg == 1:
                P.v('dve', 'memset', os_[b][:, ci, :], [], 0.0)
                continue
            pD = pss.get(); pDT = pss.get()
            for h in H:
                hs = slice(h * 64, (h + 1) * 64)
                P.mm(pD[:64, hs], G3[:, h, :], C["NU"], start=True, stop=False)
                P.mm(pD[:64, hs], UG3[:, h, :], C["ON"], start=False, stop=False)
                P.mm(pD[:64, hs], C["ID"], C["NEG"], start=False, stop=True)
            for h in H:
                hs = slice(h * 64, (h + 1) * 64)
                P.mm(pDT[:64, hs], G3[:, h, :], C["U"], start=True, stop=False)
                P.mm(pDT[:64, hs], UG3[:, h, :], C["NON"], start=False, stop=False)
                P.mm(pDT[:64, hs], C["ID"], C["NEGT"], start=False, stop=True)
            D3 = D3_r.get(); DT3 = DT3_r.get()
            P.act(D3[:], pD[:64, :].rearrange("p (h j) -> p h j", h=8), AF.Exp)
            P.act(DT3[:], pDT[:64, :].rearrange("p (h j) -> p h j", h=8), AF.Exp)
            if dbg == 2:
                P.v('dve', 'memset', os_[b][:, ci, :], [], 0.0)
                continue
            pK = pss.get(); pKQ = pss.get()
            for h in H:
                hs = slice(h * 64, (h + 1) * 64)
                P.mm(pK[:64, hs], kTc[:, h, :], kTc[:, h, :])
            for h in H:
                hs = slice(h * 64, (h + 1) * 64)
                P.mm(pKQ[:64, hs], kTc[:, h, :], qTc[:, h, :])
            t1 = t1_r.get(); nbSM = nbSM_r.get(); X = X_r.get(); aqk = aqk_r.get()
            P.v("dve", "tensor_tensor", t1[:], [pK[:64, :].rearrange("p (h j) -> p h j", h=8), D3[:]], ALU.mult)
            P.v("dve", "tensor_tensor", nbSM[:], [C["SM3"], bc(nb[:], 64)], ALU.mult)
            P.v("dve", "tensor_tensor", X[:], [t1[:], nbSM[:]], ALU.mult)
            P.v("dve", "tensor_tensor", aqk[:64], [pKQ[:64, :].rearrange("p (h j) -> p h j", h=8), DT3[:]], ALU.mult)
            if dbg == 3:
                P.v('dve', 'memset', os_[b][:, ci, :], [], 0.0)
                continue
            pT = pss.get()
            for h in H:
                P.mm(pT[:64, h * 64:(h + 1) * 64], X[:, h, :], C["ID"])
            XT = XT_r.get(); TT = TT_r.get()
            pT3 = pT[:64, :].rearrange("p (h j) -> p h j", h=8)
            P.act(XT[:], pT3, AF.Copy)
            P.v("dve", "tensor_tensor", TT[:], [XT[:], C["ID3"]], ALU.add)
            if dbg == 4:
                P.v('dve', 'memset', os_[b][:, ci, :], [], 0.0)
                continue
            for lv in range(5):
                pX = pss.get()
                for h in H:
                    P.mm(pX[:64, h * 64:(h + 1) * 64], XT[:, h, :], X[:, h, :])
                X2 = X_r.get()
                P.act(X2[:], pX[:64, :].rearrange("p (h j) -> p h j", h=8), AF.Copy)
                if lv < 4:
                    pXT = pss.get()
                    for h in H:
                        P.mm(pXT[:64, h * 64:(h + 1) * 64], X[:, h, :], XT[:, h, :])
                    XT2 = XT_r.get()
                    P.act(XT2[:], pXT[:64, :].rearrange("p (h j) -> p h j", h=8), AF.Copy)
                    XT = XT2
                X = X2
                pTT = pss.get()
                for h in H:
                    P.mm(pTT[:64, h * 64:(h + 1) * 64], X[:, h, :], TT[:, h, :])
                TT2 = TT_r.get()
                P.v("dve", "tensor_tensor", TT2[:], [pTT[:64, :].rearrange("p (h j) -> p h j", h=8), TT[:]], ALU.add)
                TT = TT2
            if dbg == 5:
                P.v('dve', 'memset', os_[b][:, ci, :], [], 0.0)
                continue
            vb = vb_r.get(); kba = kba_r.get(); kd = kd_r.get(); dA = dA_r.get()
            P.v("dve", "tensor_tensor", vb[:], [v_c, bc(be_c, 128)], ALU.mult)
            P.v("pool", "tensor_tensor", kba[:], [k_c, bc(ba[:], 128)], ALU.mult)
            P.v("pool", "tensor_tensor", kd[:], [k_c, bc(kdec[:], 128)], ALU.mult)
            P.v("dve", "tensor_tensor", dA[:], [C["ID3"], bc(a[:], 64)], ALU.mult)
            if dbg == 6:
                P.v('dve', 'memset', os_[b][:, ci, :], [], 0.0)
                continue
            pU = psb.get(); pW = pss.get(); pA = pss.get()
            for h in H:
                P.mm(pU[:64, h * 128:(h + 1) * 128], TT[:, h, :], vb[:, h, :])
            for h in H:
                P.mm(pW[:, h * 64:(h + 1) * 64], kba[:, h, :], TT[:, h, :])
            for h in H:
                P.mm(pA[:, h * 64:(h + 1) * 64], on128[:], dA[:, h, :])
            u = u_r.get(); wT = wT_r.get(); qd = qd_r.get()
            P.act(u[:], pU[:64, :].rearrange("p (h d) -> p h d", h=8), AF.Copy)
            P.act(wT[:], pW[:, :].rearrange("p (h j) -> p h j", h=8), AF.Copy)
            P.v("dve", "tensor_tensor", qd[:], [qTc, pA[:, :].rearrange("p (h j) -> p h j", h=8)], ALU.mult)
            if dbg == 7:
                P.v('dve', 'memset', os_[b][:, ci, :], [], 0.0)
                continue
            pWS = psb.get()
            for h in H:
                P.mm(pWS[:64, h * 128:(h + 1) * 128], wT[:, h, :], S[:, h, :])
            vn = vn_r.get()
            P.v("dve", "tensor_tensor", vn[:64], [u[:], pWS[:64, :].rearrange("p (h d) -> p h d", h=8)], ALU.subtract)
            if dbg == 8:
                P.v('dve', 'memset', os_[b][:, ci, :], [], 0.0)
                continue
            pO = psb.get()
            for h in H:
                P.mm(pO[:64, h * 128:(h + 1) * 128], qd[:, h, :], S[:, h, :], start=True, stop=False)
                P.mm(pO[:64, h * 128:(h + 1) * 128], aqk[:, h, :], vn[:, h, :], start=False, stop=True)
            P.act(os_[b][:, ci, :], pO[:64, :], AF.Copy)
            if dbg == 9:
                P.v('dve', 'memset', os_[b][:, ci, :], [], 0.0)
                continue
            pS = psb.get()
            for h in H:
                P.mm(pS[:, h * 128:(h + 1) * 128], kd[:, h, :], vn[:64, h, :])
            sd = sd_r.get()
            P.v("dve", "tensor_tensor", sd[:], [S[:], bc(dl[:], 128)], ALU.mult)
            P.v("dve", "tensor_tensor", S[:], [pS[:, :].rearrange("p (h d) -> p h d", h=8), sd[:]], ALU.add)
        P.dma(ov[:, c0:c0 + n, :], os_[b][:, :n, :], is_output=True)
    return P


def build_k3_lru(T=T_ALL, NT=512):
    P = Prog()
    xcT = P.dram("xcT", [1024, T]); wr = P.dram("wr", [1024, 128]); wi = P.dram("wi", [1024, 128])
    tabs = P.dram("tabs", [128, 24])
    hT = P.dram("hT", [1024, T], kind="ExternalOutput")
    tb = P.sb("tb", [128, 24]); P.dma(tb[:], tabs)
    wrs = P.sb("wrs", [128, 8, 128]); wis = P.sb("wis", [128, 8, 128])
    P.dma(wrs[:], wr.rearrange("(g i) j -> i g j", i=128)); P.dma(wis[:], wi.rearrange("(g i) j -> i g j", i=128))
    cl = P.sb("cl", [128, 8]); cl2 = P.sb("cl2", [128, 8])
    P.act(cl[:], tb[:, 16:24], AF.Exp, scale=-1.0)
    P.act(cl[:], cl[:], AF.Ln, bias=1.0)
    P.v("dve", "tensor_scalar", cl2[:], [cl[:]], -16.0, None, ALU.mult)
    P.v("dve", "tensor_scalar", cl[:], [cl[:]], -8.0, None, ALU.mult)
    carry = P.sb("carry", [128, 8]); P.v("dve", "memset", carry[:], [], 0.0)
    xs = Rot([P.sb(f"x{i}", [128, 8, NT]) for i in range(2)])
    hs = Rot([P.sb(f"ho{i}", [128, 8, NT]) for i in range(2)])
    psp = psum_pool(P, 6)
    R = lambda nm, n: Rot([P.sb(f"{nm}{i}", [128, NT]) for i in range(n)])
    r_r, ig_r, a_r, a2_r, b_r = R("r", 2), R("ig", 2), R("a", 2), R("a2", 2), R("b", 2)
    xv = xcT.rearrange("(g p) t -> p g t", p=128); hv = hT.rearrange("(g p) t -> p g t", p=128)
    nt_ = (T + NT - 1) // NT
    for t in range(nt_):
        n = min(NT, T - t * NT)
        x = xs.get(); ho = hs.get()
        P.dma(x[:, :, :n], xv[:, :, t * NT:t * NT + n])
        for g in range(8):
            pr = psp.get(); pi = psp.get()
            P.mm(pr[:, :n], wrs[:, g, :], x[:, g, :n]); P.mm(pi[:, :n], wis[:, g, :], x[:, g, :n])
            r = r_r.get(); ig = ig_r.get(); a = a_r.get(); a2 = a2_r.get(); b = b_r.get()
            P.act(r[:, :n], pr[:, :n], AF.Sigmoid, bias=tb[:, g:g + 1])
            P.act(ig[:, :n], pi[:, :n], AF.Sigmoid, bias=tb[:, 8 + g:9 + g])
            P.act(a[:, :n], r[:, :n], AF.Exp, scale=cl[:, g:g + 1])
            P.act(a2[:, :n], r[:, :n], AF.Exp, scale=cl2[:, g:g + 1])
            P.v("dve", "tensor_scalar", a2[:, :n], [a2[:, :n]], -1.0, 1.0, ALU.mult, ALU.add)
            P.act(a2[:, :n], a2[:, :n], AF.Sqrt)
            P.v("dve", "tensor_tensor", b[:, :n], [ig[:, :n], x[:, g, :n]], ALU.mult)
            P.v("dve", "tensor_tensor", b[:, :n], [b[:, :n], a2[:, :n]], ALU.mult)
            P.v("dve", "tensor_tensor_scan", ho[:, g, :n], [a[:, :n], b[:, :n]], carry[:, g:g + 1], ALU.mult, ALU.add,
                rk=[_k(a), _k(b), ("carry", g)], wk=[(ho.name, g)])
            P.v("dve", "tensor_copy", carry[:, g:g + 1], [ho[:, g, n - 1:n]], rk=[(ho.name, g)], wk=[("carry", g)])
        P.dma(hv[:, :, t * NT:t * NT + n], ho[:, :, :n], is_output=True)
    return P


def build_k3_hg(nchunk=NCHUNK):
    P = Prog()
    T = nchunk * 64
    qT = P.dram("qT", [1024, T]); kT = P.dram("kT", [1024, T])
    k_tm = P.dram("k_tm", [T, 1024]); v_tm = P.dram("v_tm", [T, 1024]); lf_tm = P.dram("lf_tm", [T, 1024])
    o_tm = P.dram("o_tm", [T, 1024], kind="ExternalOutput")
    C = load_consts(P)
    S = P.sb("S", [128, 8, 128]); P.v("dve", "memset", S[:], [], 0.0)
    qTv = qT.rearrange("(h p) t -> p h t", p=128); kTv = kT.rearrange("(h p) t -> p h t", p=128)
    kv = k_tm.rearrange("(c p) f -> p c f", p=64); vv = v_tm.rearrange("(c p) f -> p c f", p=64)
    lv = lf_tm.rearrange("(c p) f -> p c f", p=64); ov = o_tm.rearrange("(c p) f -> p c f", p=64)
    NB = 2
    qTs = [P.sb(f"qTs{i}", [128, 8, GRP * 64]) for i in range(NB)]
    kTs = [P.sb(f"kTs{i}", [128, 8, GRP * 64]) for i in range(NB)]
    ks = [P.sb(f"ks{i}", [64, GRP, 1024]) for i in range(NB)]
    ls = [P.sb(f"ls{i}", [64, GRP, 1024]) for i in range(NB)]
    vs = [P.sb(f"vs{i}", [128, GRP, 1024]) for i in range(NB)]
    os_ = [P.sb(f"os{i}", [64, GRP, 1024]) for i in range(NB)]
    for t_ in vs:
        P.v("dve", "memset", t_[64:128, :, :], [], 0.0, wk=[(t_.name, "pad")])
    psp = psum_pool(P, 8)

    def R(name, shape, n):
        return Rot([P.sb(f"{name}{i}", shape) for i in range(n)])

    ref_r = R("ref", [128, 1], 3); nref_r = R("nref", [128, 1], 3)
    E1_r, E2_r, E3_r = R("E1", [128, 64], 2), R("E2", [128, 64], 2), R("E3", [128, 64], 3)
    E4_r = R("E4", [64, 128], 2)
    qt_r, kt_r, qg_r = R("qt", [128, 64], 2), R("kt", [128, 64], 2), R("qg", [128, 64], 2)
    sc_r = R("sc", [128, 64], 2); kg_r = R("kg", [64, 128], 2); sd_r = R("sd", [128, 128], 2)
    for t_ in sc_r.items:
        P.v("dve", "memset", t_[:], [], 0.0)
    ngrp = (nchunk + GRP - 1) // GRP

    def load_group(gi):
        b = gi % NB
        c0 = gi * GRP; n = min(GRP, nchunk - c0)
        P.dma(qTs[b][:, :, :n * 64], qTv[:, :, c0 * 64:(c0 + n) * 64])
        P.dma(kTs[b][:, :, :n * 64], kTv[:, :, c0 * 64:(c0 + n) * 64])
        P.dma(ks[b][:, :n, :], kv[:, c0:c0 + n, :])
        P.dma(ls[b][:, :n, :], lv[:, c0:c0 + n, :])
        P.dma(vs[b][:64, :n, :], vv[:, c0:c0 + n, :], wk=[(vs[b].name, "data")])

    load_group(0)
    for gi in range(ngrp):
        if gi + 1 < ngrp:
            load_group(gi + 1)
        b = gi % NB
        c0 = gi * GRP; n = min(GRP, nchunk - c0)
        for ci in range(n):
            for h in range(8):
                hs = slice(h * 128, (h + 1) * 128)
                kTh = kTs[b][:, h, ci * 64:(ci + 1) * 64]; qTh = qTs[b][:, h, ci * 64:(ci + 1) * 64]
                k_h = ks[b][:, ci, hs]; lf_h = ls[b][:, ci, hs]
                pG = psp.get()
                P.mm(pG[:, 0:64], lf_h, C["U"])
                P.mm(pG[:64, 64:192], C["SU"], lf_h)
                ref = ref_r.get(); nref = nref_r.get()
                P.act(ref[:], pG[:, 32:33], AF.Copy)
                P.act(nref[:], pG[:, 32:33], AF.Copy, scale=-1.0)
                E1 = E1_r.get(); E2 = E2_r.get(); E3 = E3_r.get(); E4 = E4_r.get()
                P.act(E1[:], pG[:, 0:64], AF.Exp, bias=nref[:, 0:1])
                P.act(E2[:], pG[:, 0:64], AF.Exp, bias=ref[:, 0:1], scale=-1.0)
                P.act(E3[:], pG[:, 0:64], AF.Exp)
                P.act(E4[:], pG[:64, 64:192], AF.Exp)
                qt = qt_r.get(); kt = kt_r.get(); qg = qg_r.get(); kg = kg_r.get()
                P.v("dve", "tensor_tensor", qt[:], [qTh, E1[:]], ALU.mult)
                P.v("dve", "tensor_tensor", kt[:], [kTh, E2[:]], ALU.mult)
                P.v("pool", "tensor_tensor", qg[:], [qTh, E3[:]], ALU.mult)
                P.v("pool", "tensor_tensor", kg[:], [k_h, E4[:]], ALU.mult)
                pS = psp.get()
                P.mm(pS[:64, 0:64], kt[:], qt[:])
                sc = sc_r.get()
                P.v("dve", "tensor_tensor", sc[:64, :], [pS[:64, 0:64], C["IM"]], ALU.mult)
                Sh = S[:, h, :]
                pR = psp.get()
                P.mm(pR[:64, 0:128], sc[:], vs[b][:, ci, hs], start=True, stop=False,
                     rk=[_k(sc), (vs[b].name, "pad"), (vs[b].name, "data")])
                P.mm(pR[:64, 0:128], qg[:], Sh, start=False, stop=True, rk=[_k(qg), ("S", h)])
                pR2 = psp.get()
                P.mm(pR2[:, 128:256], kg[:], vs[b][:64, ci, hs], rk=[_k(kg), (vs[b].name, "data")])
                P.act(os_[b][:, ci, hs], pR[:64, 0:128], AF.Copy, wk=[(os_[b].name, (ci, h))])
                sd = sd_r.get()
                P.act(sd[:], Sh, AF.Copy, scale=E3[:, 63:64], rk=[("S", h), _k(E3)])
                P.v("dve", "tensor_tensor", Sh, [pR2[:, 128:256], sd[:]], ALU.add, rk=[_k(pR2), _k(sd)], wk=[("S", h)])
        P.dma(ov[:, c0:c0 + n, :], os_[b][:, :n, :], is_output=True)
    return P


def build_k4a(variant, ntiles_ctx, ntiles_lat, NT=256):
    P = Prog()
    ntok = (ntiles_ctx + ntiles_lat) * NT
    oaT = P.dram("oaT", [1024, ntok]); obT = P.dram("obT", [1024, ntok]); zT = P.dram("zT", [1024, ntok])
    xT = P.dram("xT", [1024, ntok]); wout = P.dram("wout", [1024, 1024])
    tabs = P.dram("tabs", [128, 97])
    outT = P.dram("outT", [1024, ntok], kind="ExternalOutput")
    tb = P.sb("tb", [128, 97]); P.dma(tb[:], tabs)
    ones = make_const(P, "ones", 1.0)
    Wo = load_w_bf16(P, "Wo", wout, 8, 1024)
    NB = 2
    oas = [P.sb(f"oa{i}", [128, 8, NT]) for i in range(NB)]
    obs_ = [P.sb(f"ob{i}", [128, 8, NT]) for i in range(NB)]
    zs = [P.sb(f"z{i}", [128, 8, NT]) for i in range(NB)]
    xs = [P.sb(f"x{i}", [128, 8, NT]) for i in range(NB)]
    xo = P.sb("xo", [128, 8, NT])
    sq = P.sb("sq", [128, 8, NT], BF16)
    y = P.sb("y", [128, 8, NT], BF16)
    rs_r = Rot([P.sb(f"rs{i}", [128, NT]) for i in range(2)])
    y1_r = Rot([P.sb(f"y1{i}", [128, NT]) for i in range(2)])
    psp = psum_pool(P, 6, w=NT)
    views = [a.rearrange("(k p) t -> p k t", p=128) for a in (oaT, obT, zT, xT)]
    ov = outT.rearrange("(k p) t -> p k t", p=128)
    tiles = [(i, 1) for i in range(ntiles_ctx)] + [(ntiles_ctx + i, 0) for i in range(ntiles_lat)]

    def load(ti):
        b = ti % NB; t = tiles[ti][0]
        for buf, vw in zip((oas[b], obs_[b], zs[b], xs[b]), views):
            P.dma(buf[:], vw[:, :, t * NT:(t + 1) * NT])

    load(0)
    for ti, (t, s) in enumerate(tiles):
        if ti + 1 < len(tiles):
            load(ti + 1)
        b = ti % NB
        oa, ob, z, x = oas[b], obs_[b], zs[b], xs[b]
        base = s * 48
        P.v("pool", "tensor_tensor", oa[:], [oa[:], ob[:]], ALU.add)
        if variant == "norm":
            P.act(sq[:], oa[:], AF.Square)
            P.act(z[:], z[:], AF.Silu)
            for h in range(8):
                ps = psp.get()
                P.mm(ps[:, :NT], ones[:], sq[:, h, :])
                rs = rs_r.get(); y1 = y1_r.get()
                P.act(rs[:], ps[:, :NT], AF.Sqrt, scale=1.0 / 128.0, bias=EPS)
                P.v("dve", "reciprocal", rs[:], [rs[:]])
                P.v("dve", "tensor_tensor", y1[:], [oa[:, h, :], rs[:]], ALU.mult)
                P.v("dve", "scalar_tensor_tensor", y[:, h, :], [y1[:]], tb[:, 96:97], z[:, h, :], ALU.mult, ALU.mult,
                    wk=[("y", h)])
        else:
            for h in range(8):
                P.v("dve", "tensor_tensor", y[:, h, :], [z[:, h, :], oa[:, h, :]], ALU.mult, wk=[("y", h)])
        for mo in range(8):
            py = psp.get()
            for k in range(8):
                P.mm(py[:, :NT], Wo[:, k, mo * 128:(mo + 1) * 128], y[:, k, :], start=(k == 0), stop=(k == 7),
                     rk=[("Wo", k), ("y", k)])
            P.v("dve", "scalar_tensor_tensor", xo[:, mo, :], [py[:, :NT]], tb[:, base + 16 + mo:base + 17 + mo], x[:, mo, :],
                ALU.mult, ALU.add, wk=[("xo", mo)])
        P.dma(ov[:, :, t * NT:(t + 1) * NT], xo[:], is_output=True)
    return P


_NC = {}


def get_nc(key, builder):
    if key not in _NC:
        _NC[key] = builder().finish()
    return _NC[key]


def run_nc(nc, maps):
    return run_bass_kernel_spmd(nc, maps, core_ids=list(range(NCORE))).results


def to_scan(a, col):
    if not col:
        return a
    return a.reshape(128, 64, -1).transpose(1, 0, 2).reshape(8192, -1)


def from_scan(a, col):
    if not col:
        return a
    return a.reshape(64, 128, -1).transpose(1, 0, 2).reshape(8192, -1)


def k2_input(seq_ctx, seq_lat, hf, conv):
    cols = []
    for (ioff, nc_, ooff, n, isctx, hl, hr) in k2_tiles(conv):
        seq = seq_ctx if isctx else seq_lat
        base = hf * 128 + ooff if isctx else hf * 4096 + (ooff - 128)
        lo, hi = (base - 2, base + n + 1) if conv else (base, base + n)
        blk = np.zeros((hi - lo, 1024), np.float32)
        a, b_ = max(lo, 0), min(hi, len(seq))
        blk[a - lo:b_ - lo] = seq[a:b_]
        cols.append(blk)
    return np.ascontiguousarray(np.concatenate(cols, 0).T)


def seq_dir(pc, pl, d):
    if d:
        pc, pl = pc[:, ::-1], pl[:, ::-1]
    return np.ascontiguousarray(np.concatenate([pc, pl], 1))


def run_layer(i, XL, XC, mod, inp):
    kind, j, col, last = i % 3, i // 3, i % 2 == 1, i == 3
    kname = ("gdn", "lru", "hg")[kind]
    conv = kname != "hg"
    nc2 = get_nc(("k2", kname, i if kname == "hg" else 0), lambda: build_k2(kname, hg_layer=i))
    maps = []
    for c in range(NCORE):
        b, hf = c // 2, c % 2
        m = {"xT": k2_input(XC[b], to_scan(XL[b], col), hf, conv),
             "tabs": np.ascontiguousarray(np.concatenate([mod_table(mod, i, b), mod_table(mod, i, 4),
                                                          vec_table(inp["norm_mix"][i])], 1)),
             "hmask": np.ascontiguousarray(np.tile(np.array([[hf, 1 - hf, hf, 1 - hf]], np.float32), (128, 1)))}
        if kname == "gdn":
            m["w"] = inp["gdn_w_in"][j]
            m["convw"] = np.ascontiguousarray(inp["gdn_conv"][j].reshape(4, 24, 128).transpose(2, 0, 1).reshape(128, 96))
            m["ident"] = np.eye(128, dtype=np.float32)
            ab = np.zeros((32, 2), np.float32)
            ab[:16, 0] = inp["gdn_dt_bias"][j].reshape(16); ab[:16, 1] = inp["gdn_a_log"][j].reshape(16)
            m["abtab"] = ab
        elif kname == "lru":
            m["w"] = inp["lru_w_in"][j]
            m["convw"] = np.ascontiguousarray(inp["lru_conv_w"][j].reshape(4, 8, 128).transpose(2, 0, 1).reshape(128, 32))
            m["ident"] = np.eye(128, dtype=np.float32)
            m["convb"] = vec_table(inp["lru_conv_b"][j])
        else:
            m["w"] = inp["hg_w_in"][j]
            m["lbl_in"] = np.ascontiguousarray(inp["hg_lb_logits"].reshape(4, 8, 128).transpose(2, 1, 0).reshape(128, 32))
        maps.append(m)
    res = run_nc(nc2, maps)
    rows = K2_ROWS[kname]
    PC = np.empty((4, rows, 256), np.float32); PL = np.empty((4, rows, 8192), np.float32)
    for c in range(NCORE):
        b, hf = c // 2, c % 2
        o = res[c]["outT"]
        PC[b][:, hf * 128:(hf + 1) * 128] = o[:, :128]
        PL[b][:, hf * 4096:(hf + 1) * 4096] = o[:, 128:]
    del res
    cst = chunk_consts(); gcst = gdn_consts()
    maps = []
    for c in range(NCORE):
        b, d = c // 2, c % 2
        if kname == "gdn":
            qT = seq_dir(PC[b][0:1024], PL[b][0:1024], d); kT = seq_dir(PC[b][1024:2048], PL[b][1024:2048], d)
            vT = seq_dir(PC[b][2048:3072], PL[b][2048:3072], d)
            g = seq_dir(PC[b][4096 + d * 8:4104 + d * 8], PL[b][4096 + d * 8:4104 + d * 8], d)
            be = seq_dir(PC[b][4144 + d * 8:4152 + d * 8], PL[b][4144 + d * 8:4152 + d * 8], d)
            maps.append({"qT": qT, "kT": kT, "k_tm": np.ascontiguousarray(kT.T), "v_tm": np.ascontiguousarray(vT.T),
                         "g_tm": np.ascontiguousarray(g.T), "b_tm": np.ascontiguousarray(be.T), "cst": gcst})
        elif kname == "lru":
            tabs = np.concatenate([vec_table(inp["lru_b_r"][j, d]), vec_table(inp["lru_b_i"][j, d]),
                                   vec_table(inp["lru_lambda"][j, d])], 1)
            maps.append({"xcT": seq_dir(PC[b][1024:2048], PL[b][1024:2048], d),
                         "wr": np.ascontiguousarray(inp["lru_w_r"][j, d].reshape(1024, 128)),
                         "wi": np.ascontiguousarray(inp["lru_w_i"][j, d].reshape(1024, 128)),
                         "tabs": np.ascontiguousarray(tabs)})
        else:
            r0 = 1024 + d * 2048
            qT = seq_dir(PC[b][0:1024], PL[b][0:1024], d); kT = seq_dir(PC[b][r0:r0 + 1024], PL[b][r0:r0 + 1024], d)
            lfT = seq_dir(PC[b][r0 + 1024:r0 + 2048], PL[b][r0 + 1024:r0 + 2048], d)
            vT = seq_dir(PC[b][5120:6144], PL[b][5120:6144], d)
            maps.append({"qT": qT, "kT": kT, "k_tm": np.ascontiguousarray(kT.T), "v_tm": np.ascontiguousarray(vT.T),
                         "lf_tm": np.ascontiguousarray(lfT.T), "cst": cst})
    nc3 = get_nc(("k3", kname), {"gdn": build_k3_gdn, "lru": build_k3_lru, "hg": build_k3_hg}[kname])
    res = run_nc(nc3, maps)
    OC = np.empty((2, 4, 256, 1024), np.float32); OL = np.empty((2, 4, 8192, 1024), np.float32)
    for c in range(NCORE):
        b, d = c // 2, c % 2
        o = res[c]["hT"].T if kname == "lru" else res[c]["o_tm"]
        oc, ol = o[:256], o[256:]
        if d:
            oc, ol = oc[::-1], ol[::-1]
        OC[d, b] = oc
        OL[d, b] = from_scan(ol, col)
    del res
    zr = {"gdn": (3072, 4096), "lru": (0, 1024), "hg": (6144, 7168)}[kname]
    variant = "lru" if kname == "lru" else "norm"
    with_ctx = not last
    nc4 = get_nc(("k4a", variant, with_ctx), lambda: build_k4a(variant, 1 if with_ctx else 0, 16))
    normw = {"gdn": lambda: inp["gdn_norm"][j], "hg": lambda: inp["hg_norm"][j], "lru": lambda: np.ones(128, np.float32)}[kname]()
    wout = {"gdn": "gdn_w_out", "lru": "lru_w_out", "hg": "hg_w_out"}[kname]
    maps = []
    for c in range(NCORE):
        b, hf = c // 2, c % 2
        cs, ls = slice(hf * 128, (hf + 1) * 128), slice(hf * 4096, (hf + 1) * 4096)
        zc = PC[b][zr[0]:zr[1]].T
        zl = from_scan(np.ascontiguousarray(PL[b][zr[0]:zr[1]].T), col)
        tabs = np.concatenate([mod_table(mod, i, b), mod_table(mod, i, 4), np.asarray(normw, np.float32).reshape(128, 1)], 1)
        maps.append({"oaT": pack_tok(OC[0, b][cs], OL[0, b][ls], with_ctx), "obT": pack_tok(OC[1, b][cs], OL[1, b][ls], with_ctx),
                     "zT": pack_tok(zc[cs], zl[ls], with_ctx), "xT": pack_tok(XC[b][cs], XL[b][ls], with_ctx),
                     "wout": inp[wout][j], "tabs": np.ascontiguousarray(tabs)})
    res = run_nc(nc4, maps)
    XL2, XC2 = np.empty_like(XL), XC.copy()
    for c in range(NCORE):
        unpack_tok(res[c]["outT"], XL2, XC2, c, with_ctx)
    return XL2, XC2


def kernel(**inputs):
    inp = {k: np.asarray(v) for k, v in inputs.items()}
    mod = run_k1(inp)
    XL = np.ascontiguousarray(inp["x"], dtype=np.float32)
    XC = np.ascontiguousarray(inp["ctx"], dtype=np.float32)
    for i in range(4):
        XL, XC = run_layer(i, XL, XC, mod, inp)
        XL, XC = run_ffn(XL, XC, mod, i, inp, final=(i == 3))
    return XL
```

```python
from contextlib import ExitStack
import numpy as np
import concourse.bass as bass
import concourse.mybir as mybir
from concourse.bass_utils import run_bass_kernel_spmd

F32 = mybir.dt.float32
BF16 = mybir.dt.bfloat16
AF = mybir.ActivationFunctionType
ALU = mybir.AluOpType
AX = mybir.AxisListType
ENGS = ("pe", "act", "dve", "pool", "sp")
EPS = 1e-6
NCORE = 8


def _k(ap):
    if ap is None or isinstance(ap, (int, float)):
        return None
    return (ap.name, None)


def _norm(k):
    if k is None:
        return None
    if isinstance(k, str):
        return (k, None)
    return k


class Prog:
    EPOCH = 20000
    NDMA = 24

    def __init__(self):
        self.nc = bass.Bass("TRN2", target_bir_lowering=False)
        self.stack = ExitStack()
        self.ops = {e: [] for e in ENGS}
        self.seq = {e: 0 for e in ENGS}
        self.track = {}
        self.waited = {e: {} for e in ENGS}
        self.dma_k = 0
        self.out_tokens = []
        self.cnt = 0
        self.needed = {e: set() for e in ENGS}

    def sb(self, name, shape, dt=F32):
        return self.stack.enter_context(self.nc.sbuf_tensor(name, list(shape), dt))

    def ps(self, name, shape, dt=F32):
        return self.stack.enter_context(self.nc.psum_tensor(name, list(shape), dt))

    def dram(self, name, shape, dt=F32, kind="ExternalInput"):
        return self.nc.dram_tensor(name, list(shape), dt, kind=kind).ap()

    def _states(self, key):
        name, sub = key
        d = self.track.get(name)
        if d is None:
            return []
        if sub is None:
            return list(d.values())
        out = []
        if None in d:
            out.append(d[None])
        if sub in d:
            out.append(d[sub])
        return out

    def _emit(self, e, fn, reads, writes, dma=False):
        reads = [_norm(k) for k in reads if k is not None]
        writes = [_norm(k) for k in writes if k is not None]
        deps = {}

        def add(d):
            for sk, v in d.items():
                if e == "pe" and sk == ("eng", "pe") and not dma:
                    continue
                if deps.get(sk, 0) < v:
                    deps[sk] = v

        for k in reads:
            for st in self._states(k):
                add(st["w"])
        for k in writes:
            for st in self._states(k):
                add(st["w"])
                add(st["r"])
        if dma:
            slot = self.dma_k % self.NDMA
            cnt = self.dma_k // self.NDMA + 1
            self.dma_k += 1
            if cnt > 1:
                sk = ("dma", slot)
                deps[sk] = max(deps.get(sk, 0), cnt - 1)
            sk, v = ("dma", slot), cnt
        else:
            self.seq[e] += 1
            sk, v = ("eng", e), self.seq[e]
        waits = []
        wd = self.waited[e]
        for dk, dv in deps.items():
            if wd.get(dk, 0) < dv:
                wd[dk] = dv
                waits.append((dk, dv))
                if dk[0] == "eng":
                    self.needed[dk[1]].add(dv)
        self.ops[e].append((fn, waits, sk, v))
        self.cnt += 1
        for name, sub in reads:
            d = self.track.setdefault(name, {})
            st = d.setdefault(sub, {"w": {}, "r": {}})
            if st["r"].get(sk, 0) < v:
                st["r"][sk] = v
        for name, sub in writes:
            d = self.track.setdefault(name, {})
            if sub is None:
                d.clear()
            d[sub] = {"w": {sk: v}, "r": {}}
        return (sk, v)

    def _keys(self, aps, override):
        if override is not None:
            return list(override)
        return [_k(a) for a in aps]

    def mm(self, out, lhsT, rhs, start=True, stop=True, rk=None, wk=None):
        return self._emit("pe", lambda eng: eng.matmul(out, lhsT, rhs, start=start, stop=stop),
                          self._keys([lhsT, rhs], rk), self._keys([out], wk))

    def transpose(self, out, in_, ident, rk=None, wk=None):
        return self._emit("pe", lambda eng: eng.transpose(out, in_, ident),
                          self._keys([in_, ident], rk), self._keys([out], wk))

    def act(self, out, in_, func, bias=0.0, scale=1.0, rk=None, wk=None):
        return self._emit("act", lambda e: e.activation(out, in_, func, bias=bias, scale=scale),
                          self._keys([in_, bias, scale], rk), self._keys([out], wk))

    def v(self, eng, meth, out, ins, *args, rk=None, wk=None, **kw):
        rextra = [a for a in list(args) + list(kw.values()) if hasattr(a, "name") and hasattr(a, "shape")]
        return self._emit(eng, lambda e: getattr(e, meth)(out, *ins, *args, **kw),
                          self._keys(list(ins) + rextra, rk), self._keys([out], wk))

    def dma(self, out, in_, eng="sp", rk=None, wk=None, is_output=False):
        tok = self._emit(eng, lambda e: e.dma_start(out=out, in_=in_),
                         self._keys([in_], rk), self._keys([out], wk), dma=True)
        if is_output:
            self.out_tokens.append(tok)
        return tok

    def finish(self):
        nc = self.nc
        E = self.EPOCH
        rank = {e: {sq: i for i, sq in enumerate(sorted(self.needed[e]))} for e in ENGS}

        def semval(sk, v):
            if sk[0] == "dma":
                return sk, 16 * v
            r = rank[sk[1]][v]
            return ("eng", sk[1], r // E), r % E + 1

        fin = {}
        for sk, v in self.out_tokens:
            fin[sk] = max(fin.get(sk, 0), v)
        plan = {e: [] for e in ENGS}
        semkeys = set()
        for e in ENGS:
            for fn, waits, sk, v in self.ops[e]:
                w2 = [semval(a, b) for a, b in waits]
                if sk[0] == "dma":
                    inc = (sk, 16)
                elif v in rank[e]:
                    inc = (semval(sk, v)[0], 1)
                else:
                    inc = None
                plan[e].append((fn, w2, inc))
                for a, _ in w2:
                    semkeys.add(a)
                if inc:
                    semkeys.add(inc[0])
        finw = [semval(sk, v) for sk, v in fin.items()]
        for a, _ in finw:
            semkeys.add(a)
        sems = {}
        for sk in sorted(semkeys, key=str):
            sems[sk] = self.stack.enter_context(nc.semaphore("s_" + "_".join(str(x) for x in sk)))
        self.n_inc = sum(1 for e in ENGS for p in plan[e] if p[2])

        def run(e, eng):
            for fn, waits, inc in plan[e]:
                for w, v in waits:
                    eng.wait_ge(sems[w], v)
                ins = fn(eng)
                if inc:
                    ins.then_inc(sems[inc[0]], inc[1])
            if e == "sp":
                for w, v in finw:
                    eng.wait_ge(sems[w], v)

        with nc.Block() as block:
            @block.tensor
            def _(eng):
                run("pe", eng)

            @block.scalar
            def _(eng):
                run("act", eng)

            @block.vector
            def _(eng):
                run("dve", eng)

            @block.gpsimd
            def _(eng):
                run("pool", eng)

            @block.sync
            def _(eng):
                run("sp", eng)
        self.stack.close()
        return nc


class Rot:
    def __init__(self, items):
        self.items = items
        self.i = 0

    def get(self):
        x = self.items[self.i % len(self.items)]
        self.i += 1
        return x


def psum_pool(P, n, name="ps", w=512):
    return Rot([P.ps(f"{name}{i}", [128, w]) for i in range(n)])


def launch(P, in_maps):
    nc = P.finish()
    res = run_bass_kernel_spmd(nc, in_maps, core_ids=list(range(NCORE)))
    return res.results


def load_w_bf16(P, name, dram_ap, kc, cout, split=1):
    w = P.sb(name, [128, kc, cout], BF16)
    v = dram_ap.rearrange("(k p) c -> p k c", p=128)
    for k in range(kc):
        P.dma(w[:, k, :], v[:, k, :], eng="pool", wk=[(name, k)])
    return w


def make_const(P, name, val, shape=(128, 128), dt=BF16):
    t = P.sb(name, list(shape), dt)
    P.v("dve", "memset", t[:], [], val)
    return t


def norm_mod(P, x, h, nt, A, B, ones_bf, sq, psp, rstd, tmp, nk=8, dnorm=1024.0):
    P.act(sq[:, :, :nt], x[:, :, :nt], AF.Square)
    ps = psp.get()
    for k in range(nk):
        P.mm(ps[:, :nt], ones_bf[:], sq[:, k, :nt], start=(k == 0), stop=(k == nk - 1))
    P.act(rstd[:, :nt], ps[:, :nt], AF.Sqrt, scale=1.0 / dnorm, bias=EPS)
    P.v("dve", "reciprocal", rstd[:, :nt], [rstd[:, :nt]])
    tn, hn = tmp.name, h.name
    for k in range(nk):
        P.v("dve", "tensor_tensor", tmp[:, k, :nt], [x[:, k, :nt], rstd[:, :nt]], ALU.mult, wk=[(tn, k)])
        if B is None:
            P.v("dve", "tensor_scalar", h[:, k, :nt], [tmp[:, k, :nt]], A[:, k:k + 1], None, ALU.mult,
                rk=[(tn, k), _k(A)], wk=[(hn, k)])
        else:
            P.act(h[:, k, :nt], tmp[:, k, :nt], AF.Identity, scale=A[:, k:k + 1], bias=B[:, k:k + 1],
                  rk=[(tn, k), _k(A), _k(B)], wk=[(hn, k)])


def build_k1():
    P = Prog()
    cT = P.dram("cT", [128, 8 * 5])
    w = P.dram("w", [4 * 1024, 768])
    b = P.dram("b", [128, 24])
    out = P.dram("mod", [24 * 128, 5], kind="ExternalOutput")
    c_sb = P.sb("c_sb", [128, 40])
    sc = P.sb("sc", [128, 40])
    b_sb = P.sb("b_sb", [128, 24])
    P.dma(c_sb[:], cT)
    P.dma(b_sb[:], b)
    P.act(sc[:], c_sb[:], AF.Silu)
    wv = w.rearrange("(l k p) c -> l p k c", p=128, k=8)
    wsb = [P.sb(f"w{l}", [128, 8, 768]) for l in range(4)]
    for l in range(4):
        for k in range(8):
            P.dma(wsb[l][:, k, :], wv[l][:, k, :], wk=[(f"w{l}", k)])
    psp = psum_pool(P, 4, w=16)
    res = P.sb("res", [128, 24, 5])
    for l in range(4):
        for m in range(6):
            ps = psp.get()
            for k in range(8):
                P.mm(ps[:, :5], wsb[l][:, k, m * 128:(m + 1) * 128], sc[:, k * 5:(k + 1) * 5],
                     start=(k == 0), stop=(k == 7), rk=[(f"w{l}", k), _k(sc)])
            j = l * 6 + m
            P.v("dve", "tensor_scalar", res[:, j, :], [ps[:, :5]], b_sb[:, j:j + 1], None, ALU.add,
                wk=[("res", j)])
    P.dma(out.rearrange("(j p) n -> p j n", p=128), res[:], is_output=True)
    return P


def run_k1(inp):
    c5 = np.concatenate([inp["c"], inp["c_ctx"][None]], 0).astype(np.float32)
    cT = np.ascontiguousarray(c5.T.reshape(8, 128, 5).transpose(1, 0, 2)).reshape(128, 40)
    maps = []
    for c in range(NCORE):
        w = np.ascontiguousarray(inp["ada_w"][:, :, c * 768:(c + 1) * 768]).reshape(4 * 1024, 768)
        bb = inp["ada_b"][:, c * 768:(c + 1) * 768].reshape(4, 6, 128).transpose(2, 0, 1).reshape(128, 24)
        maps.append({"cT": cT, "w": w, "b": np.ascontiguousarray(bb)})
    res = launch(build_k1(), maps)
    mod = np.zeros((4, 6144, 5), np.float32)
    for c in range(NCORE):
        r = res[c]["mod"].reshape(4, 6, 128, 5)
        mod[:, c * 768:(c + 1) * 768, :] = r.reshape(4, 768, 5)
    return mod


def mod_table(mod, l, col):
    return np.ascontiguousarray(mod[l, :, col].reshape(48, 128).T)


def vec_table(v):
    v = np.asarray(v, np.float32)
    return np.ascontiguousarray(v.reshape(-1, 128).T)


FFN_H = 2816


def build_k4b(ntiles_ctx, ntiles_lat, NT=256, final=False):
    P = Prog()
    ntok = (ntiles_ctx + ntiles_lat) * NT
    xT = P.dram("xT", [1024, ntok])
    w1 = P.dram("w1", [1024, FFN_H]); w3 = P.dram("w3", [1024, FFN_H]); w2 = P.dram("w2", [FFN_H, 1024])
    tabs = P.dram("tabs", [128, 2 * 48 + 16])
    outT = P.dram("outT", [1024, ntok], kind="ExternalOutput")
    tb = P.sb("tb", [128, 112])
    P.dma(tb[:], tabs)
    ones = make_const(P, "ones", 1.0)
    AB = P.sb("AB", [128, 2, 8])
    for s in range(2):
        base = s * 48
        P.v("dve", "scalar_tensor_tensor", AB[:, s, :], [tb[:, base + 32:base + 40]], 1.0, tb[:, 96:104], ALU.add, ALU.mult,
            wk=[("AB", s)])
    W1 = load_w_bf16(P, "W1", w1, 8, FFN_H)
    W3 = load_w_bf16(P, "W3", w3, 8, FFN_H)
    W2 = load_w_bf16(P, "W2", w2, 22, 1024)
    xs = Rot([P.sb(f"x{i}", [128, 8, NT]) for i in range(2)])
    xo = P.sb("xo", [128, 8, NT])
    sq = P.sb("sq", [128, 8, NT], BF16)
    h = P.sb("h", [128, 8, NT], BF16)
    g = P.sb("g", [128, 22, NT], BF16)
    rstd = P.sb("rstd", [128, NT])
    sas = Rot([P.sb(f"sa{i}", [128, NT]) for i in range(2)])
    psp = psum_pool(P, 6, w=NT)
    xv = xT.rearrange("(k p) t -> p k t", p=128)
    ov = outT.rearrange("(k p) t -> p k t", p=128)
    nt = NT
    tiles = [(i, 1) for i in range(ntiles_ctx)] + [(ntiles_ctx + i, 0) for i in range(ntiles_lat)]
    xcur = xs.get()
    P.dma(xcur[:], xv[:, :, 0:NT])
    for ti, (t, s) in enumerate(tiles):
        x = xcur
        if ti + 1 < len(tiles):
            xcur = xs.get()
            t2 = tiles[ti + 1][0]
            P.dma(xcur[:], xv[:, :, t2 * NT:(t2 + 1) * NT])
        base = s * 48
        norm_mod(P, x, h, nt, AB[:, s, :], tb[:, base + 24:base + 32], ones, sq, psp, rstd, xo)
        for m in range(22):
            pa = psp.get(); pb = psp.get()
            for k in range(8):
                P.mm(pa[:, :nt], W1[:, k, m * 128:(m + 1) * 128], h[:, k, :], start=(k == 0), stop=(k == 7),
                     rk=[("W1", k), ("h", k)])
            for k in range(8):
                P.mm(pb[:, :nt], W3[:, k, m * 128:(m + 1) * 128], h[:, k, :], start=(k == 0), stop=(k == 7),
                     rk=[("W3", k), ("h", k)])
            sa = sas.get()
            P.act(sa[:], pa[:, :nt], AF.Silu)
            P.v("dve", "tensor_tensor", g[:, m, :], [sa[:], pb[:, :nt]], ALU.mult, wk=[("g", m)])
        for mo in range(8):
            py = psp.get()
            for k in range(22):
                P.mm(py[:, :nt], W2[:, k, mo * 128:(mo + 1) * 128], g[:, k, :], start=(k == 0), stop=(k == 21),
                     rk=[("W2", k), ("g", k)])
            P.v("dve", "scalar_tensor_tensor", xo[:, mo, :], [py[:, :nt]], tb[:, base + 40 + mo:base + 41 + mo], x[:, mo, :],
                ALU.mult, ALU.add, wk=[("xo", mo)])
        if final:
            norm_mod(P, xo, xo, nt, tb[:, 104:112], None, ones, sq, psp, rstd, x)
        P.dma(ov[:, :, t * NT:(t + 1) * NT], xo[:], is_output=True)
    return P


def tok_split(XL, XC, c):
    b, hf = c // 2, c % 2
    return XC[b, hf * 128:(hf + 1) * 128], XL[b, hf * 4096:(hf + 1) * 4096]


def pack_tok(xc, xl, with_ctx=True, cpad=256):
    C = xl.shape[1]
    if not with_ctx:
        return np.ascontiguousarray(xl.T)
    out = np.zeros((C, cpad + xl.shape[0]), np.float32)
    out[:, :xc.shape[0]] = xc.T
    out[:, cpad:] = xl.T
    return out


def unpack_tok(oT, XL, XC, c, with_ctx=True, cpad=256):
    b, hf = c // 2, c % 2
    if with_ctx:
        XC[b, hf * 128:(hf + 1) * 128] = oT[:, :128].T
        XL[b, hf * 4096:(hf + 1) * 4096] = oT[:, cpad:].T
    else:
        XL[b, hf * 4096:(hf + 1) * 4096] = oT.T


def run_ffn(XL, XC, mod, l, inp, final):
    with_ctx = not final
    P = build_k4b(1 if with_ctx else 0, 16, NT=256, final=final)
    maps = []
    for c in range(NCORE):
        xc, xl = tok_split(XL, XC, c)
        tabs = np.concatenate([mod_table(mod, l, c // 2), mod_table(mod, l, 4),
                               vec_table(inp["norm_ffn"][l]), vec_table(inp["norm_final"])], 1)
        maps.append({"xT": pack_tok(xc, xl, with_ctx), "w1": inp["ffn_w1"][l], "w3": inp["ffn_w3"][l],
                     "w2": inp["ffn_w2"][l], "tabs": np.ascontiguousarray(tabs)})
    res = launch(P, maps)
    XL2, XC2 = np.empty_like(XL), XC.copy()
    for c in range(NCORE):
        unpack_tok(res[c]["outT"], XL2, XC2, c, with_ctx)
    return XL2, XC2


def k2_tiles(conv):
    tiles = []
    if conv:
        louts = [509] * 8 + [24]
        tiles.append((0, 131, 0, 128, 1, 0, 1))
        in_off, out_off = 131, 128
        for i, n in enumerate(louts):
            tiles.append((in_off, n + 3, out_off, n, 0, 2 if i == 0 else None, 3 if i == len(louts) - 1 else None))
            in_off += n + 3
            out_off += n
    else:
        tiles.append((0, 128, 0, 128, 1, None, None))
        for i in range(8):
            tiles.append((128 + i * 512, 512, 128 + i * 512, 512, 0, None, None))
    return tiles


K2_COUT = {"gdn": 4128, "lru": 2048, "hg": 5120}
K2_ROWS = {"gdn": 4160, "lru": 2048, "hg": 7 * 1024}


def build_k2(kind, hg_layer=2):
    P = Prog()
    conv = kind in ("gdn", "lru")
    tiles = k2_tiles(conv)
    ncols_total = sum(t[1] for t in tiles)
    cout, rows_out = K2_COUT[kind], K2_ROWS[kind]
    xT = P.dram("xT", [1024, ncols_total])
    w = P.dram("w", [1024, cout])
    tabs = P.dram("tabs", [128, 104])
    hmask = P.dram("hmask", [128, 4])
    outT = P.dram("outT", [rows_out, 4224], kind="ExternalOutput")
    tb = P.sb("tb", [128, 104]); P.dma(tb[:], tabs)
    hm = P.sb("hm", [128, 4]); P.dma(hm[:], hmask)
    ones = make_const(P, "ones", 1.0)
    AB = P.sb("AB", [128, 2, 8])
    for s in range(2):
        P.v("dve", "scalar_tensor_tensor", AB[:, s, :], [tb[:, s * 48 + 8:s * 48 + 16]], 1.0, tb[:, 96:104],
            ALU.add, ALU.mult, wk=[("AB", s)])
    nconv = {"gdn": 24, "lru": 8, "hg": 0}[kind]
    if conv:
        cw_d = P.dram("convw", [128, 4 * nconv])
        cw = P.sb("cw", [128, 4 * nconv]); P.dma(cw[:], cw_d)
        identf = P.sb("identf", [128, 128])
        id_d = P.dram("ident", [128, 128]); P.dma(identf[:], id_d)
        dg = P.sb("dg", [128, 4 * nconv, 128], BF16)
        for i in range(4 * nconv):
            P.v("dve", "tensor_scalar", dg[:, i, :], [identf[:]], cw[:, i:i + 1], None, ALU.mult, wk=[("dg", i)])
    if kind == "gdn":
        ab_d = P.dram("abtab", [32, 2])
        abt = P.sb("abt", [32, 2]); P.dma(abt[:], ab_d)
        nexpA = P.sb("nexpA", [32, 1])
        P.act(nexpA[:], abt[:, 1:2], AF.Exp)
        P.v("dve", "tensor_scalar", nexpA[:], [nexpA[:]], -1.0, None, ALU.mult)
    if kind == "lru":
        cb_d = P.dram("convb", [128, 8])
        cb = P.sb("cb", [128, 8]); P.dma(cb[:], cb_d)
    if kind == "hg":
        lb_d = P.dram("lbl_in", [128, 32])
        lbl = P.sb("lbl", [128, 8, 4]); P.dma(lbl[:], lb_d.rearrange("p (c l) -> p c l", l=4))
        el = P.sb("el", [128, 8, 4])
        P.act(el[:], lbl[:], AF.Exp)
        den = P.sb("den", [128, 8]); num = P.sb("num", [128, 8])
        P.v("dve", "tensor_reduce", den[:], [el[:]], AX.X, ALU.add)
        P.v("dve", "tensor_reduce", num[:], [el[:, :, 1:hg_layer + 1]], AX.X, ALU.add)
        lb = P.sb("lb", [128, 8]); oml = P.sb("oml", [128, 8])
        P.v("dve", "reciprocal", den[:], [den[:]])
        P.v("dve", "tensor_tensor", lb[:], [num[:], den[:]], ALU.mult)
        P.v("dve", "tensor_scalar", oml[:], [lb[:]], -1.0, 1.0, ALU.mult, ALU.add)
    W = load_w_bf16(P, "W", w, 8, cout)
    xs = Rot([P.sb(f"x{i}", [128, 8, 512]) for i in range(2)])
    tmp = P.sb("tmp", [128, 8, 512])
    sq = P.sb("sq", [128, 8, 512], BF16)
    h = P.sb("h", [128, 8, 512], BF16)
    rstd = P.sb("rstd", [128, 512])
    psp = psum_pool(P, 5)
    psc = psum_pool(P, 2, "pc")
    pbs = Rot([P.sb(f"pb{i}", [128, 512], BF16) for i in range(3)])
    obs = Rot([P.sb(f"ob{i}", [128, 512]) for i in range(6)])
    sts = Rot([P.sb(f"st{i}", [128, 512]) for i in range(3)])
    sqs = Rot([P.sb(f"sqh{i}", [128, 512], BF16) for i in range(2)])
    rss = Rot([P.sb(f"rs{i}", [128, 512]) for i in range(2)])
    xv = xT.rearrange("(k p) t -> p k t", p=128)

    def store(row0, rows, off, n, ob):
        P.dma(outT[row0:row0 + rows, off:off + n], ob[:rows, :n], is_output=True)

    xcur = xs.get()
    P.dma(xcur[:, :, :tiles[0][1]], xv[:, :, 0:tiles[0][1]])
    for ti, (ioff, nc_, ooff, n, isctx, hl, hr) in enumerate(tiles):
        x = xcur
        if ti + 1 < len(tiles):
            xcur = xs.get()
            o2, n2 = tiles[ti + 1][0], tiles[ti + 1][1]
            P.dma(xcur[:, :, :n2], xv[:, :, o2:o2 + n2])
        s = isctx
        norm_mod(P, x, h, nc_, AB[:, s, :], tb[:, s * 48:s * 48 + 8], ones, sq, psp, rstd, tmp)
        nchunks = (cout + 127) // 128
        for m in range(nchunks):
            rows = min(128, cout - m * 128)
            ps = psp.get()
            for k in range(8):
                P.mm(ps[:rows, :nc_], W[:, k, m * 128:m * 128 + rows], h[:, k, :nc_], start=(k == 0), stop=(k == 7),
                     rk=[("W", k), ("h", k)])
            if m < nconv and not (kind == "lru") or (kind == "lru" and m >= 8):
                ci = m if kind == "gdn" else m - 8
                pb = pbs.get()
                P.act(pb[:, :nc_], ps[:, :nc_], AF.Copy)
                if hl is not None:
                    P.v("dve", "tensor_scalar", pb[:, 0:2], [pb[:, 0:2]], hm[:, hl:hl + 1], None, ALU.mult)
                if hr is not None:
                    P.v("dve", "tensor_scalar", pb[:, nc_ - 1:nc_], [pb[:, nc_ - 1:nc_]], hm[:, hr:hr + 1], None, ALU.mult)
                pc = psc.get()
                for j in range(4):
                    P.mm(pc[:, :n], dg[:, j * nconv + ci, :], pb[:, j:j + n], start=(j == 0), stop=(j == 3),
                         rk=[("dg", j * nconv + ci), _k(pb)])
                ob = obs.get()
                if kind == "lru":
                    P.act(ob[:, :n], pc[:, :n], AF.Identity, bias=cb[:, ci:ci + 1])
                    store(m * 128, 128, ooff, n, ob)
                elif m >= 16:
                    P.act(ob[:, :n], pc[:, :n], AF.Silu)
                    store(m * 128, 128, ooff, n, ob)
                else:
                    st = sts.get(); sqh = sqs.get(); rs = rss.get()
                    P.act(st[:, :n], pc[:, :n], AF.Silu)
                    P.v("pool", "tensor_tensor", sqh[:, :n], [st[:, :n], st[:, :n]], ALU.mult)
                    pss = psc.get()
                    P.mm(pss[:, :n], ones[:], sqh[:, :n])
                    P.act(rs[:, :n], pss[:, :n], AF.Sqrt, bias=EPS)
                    P.v("dve", "reciprocal", rs[:, :n], [rs[:, :n]])
                    P.v("dve", "scalar_tensor_tensor", ob[:, :n], [st[:, :n]], (128.0 ** -0.5) if m < 8 else 1.0, rs[:, :n],
                        ALU.mult, ALU.mult)
                    store(m * 128, 128, ooff, n, ob)
                continue
            c0 = 1 if conv else 0
            lo = 2 if conv else 0
            src = ps[:rows, lo:lo + n]
            if kind == "gdn" and m < 32:
                ob = obs.get()
                P.v("dve", "tensor_copy", ob[:, :n], [src])
                store(m * 128, 128, ooff, n, ob)
            elif kind == "gdn":
                ob = obs.get(); ob2 = obs.get()
                P.act(ob[:32, :n], src, AF.Exp, bias=abt[:, 0:1])
                P.act(ob[:32, :n], ob[:32, :n], AF.Ln, bias=1.0)
                P.v("dve", "tensor_scalar", ob[:32, :n], [ob[:32, :n]], nexpA[:, 0:1], None, ALU.mult)
                store(4096, 32, ooff, n, ob)
                P.act(ob2[:32, :n], src, AF.Sigmoid)
                store(4128, 32, ooff, n, ob2)
            elif kind == "lru":
                ob = obs.get()
                P.act(ob[:, :n], src, AF.Gelu)
                store(m * 128, 128, ooff, n, ob)
            else:
                ob = obs.get()
                if m < 8:
                    P.act(ob[:, :n], src, AF.Silu)
                    store(m * 128, 128, ooff, n, ob)
                elif m < 24:
                    d = (m - 8) // 8; c = (m - 8) % 8
                    ob2 = obs.get()
                    P.act(ob[:, :n], src, AF.Sigmoid)
                    P.v("dve", "tensor_scalar", ob[:, :n], [ob[:, :n]], oml[:, c:c + 1], lb[:, c:c + 1], ALU.mult, ALU.add)
                    P.act(ob2[:, :n], ob[:, :n], AF.Ln)
                    store(1024 + d * 2048 + 1024 + c * 128, 128, ooff, n, ob2)
                    ob3 = obs.get()
                    P.v("dve", "tensor_scalar", ob3[:, :n], [ob[:, :n]], -1.0, 1.0, ALU.mult, ALU.add)
                    store(1024 + d * 2048 + c * 128, 128, ooff, n, ob3)
                else:
                    P.v("dve", "tensor_copy", ob[:, :n], [src])
                    store(5120 + (m - 24) * 128, 128, ooff, n, ob)
    return P


T_ALL = 8448
NCHUNK = T_ALL // 64
GRP = 4
GDN_NB = 2


def chunk_consts():
    i = np.arange(64)
    U = (i[:, None] <= i[None, :]).astype(np.float32)
    SU = (i[:, None] > i[None, :]).astype(np.float32)
    NEG = np.where(i[:, None] >= i[None, :], 0.0, -30000.0).astype(np.float32)
    POS = np.where(i[None, :] >= i[:, None], 0.0, 30000.0).astype(np.float32)
    SM = (i[:, None] > i[None, :]).astype(np.float32)
    IM = (i[None, :] >= i[:, None]).astype(np.float32)
    ID = np.eye(64, dtype=np.float32)
    ON = np.ones((64, 64), np.float32)
    return np.ascontiguousarray(np.concatenate([U, SU, NEG, POS, SM, IM, ID, ON], 1))


def load_consts(P):
    cd = P.dram("cst", [64, 512])
    c = P.sb("cst_sb", [64, 512]); P.dma(c[:], cd)
    names = ["U", "SU", "NEG", "POS", "SM", "IM", "ID", "ON"]
    C = {n: c[:, i * 64:(i + 1) * 64] for i, n in enumerate(names)}
    on128 = P.sb("on128", [64, 128]); P.v("dve", "memset", on128[:], [], 1.0)
    C["ON128"] = on128
    return C


def gdn_consts():
    i = np.arange(64)
    U = (i[:, None] <= i[None, :]).astype(np.float32)
    SU = (i[:, None] > i[None, :]).astype(np.float32)
    ON = np.ones((64, 64), np.float32)
    NEG = np.where(i[:, None] >= i[None, :], 0.0, -30000.0).astype(np.float32)
    NEGT = np.where(i[None, :] >= i[:, None], 0.0, -30000.0).astype(np.float32)
    ID = np.eye(64, dtype=np.float32)
    SM = (i[:, None] > i[None, :]).astype(np.float32)
    rep = lambda m: np.tile(m, (1, 8))
    return np.ascontiguousarray(np.concatenate([U, -U, ON, -ON, NEG, NEGT, ID, SU, rep(U), rep(ID), rep(SM), rep(ON)], 1))


def build_k3_gdn(nchunk=NCHUNK, pipeline=True):
    P = Prog()
    T = nchunk * 64
    G2 = 2
    qT = P.dram("qT", [1024, T]); kT = P.dram("kT", [1024, T])
    k_tm = P.dram("k_tm", [T, 1024]); v_tm = P.dram("v_tm", [T, 1024])
    g_tm = P.dram("g_tm", [T, 8]); b_tm = P.dram("b_tm", [T, 8])
    o_tm = P.dram("o_tm", [T, 1024], kind="ExternalOutput")
    cd = P.dram("cst", [64, 2560])
    c = P.sb("cst_sb", [64, 2560]); P.dma(c[:], cd)
    cn = ["U", "NU", "ON", "NON", "NEG", "NEGT", "ID", "SU"]
    C = {n: c[:, i * 64:(i + 1) * 64] for i, n in enumerate(cn)}
    for i, n in enumerate(["U3", "ID3", "SM3", "ON3"]):
        C[n] = c[:, 512 + i * 512:512 + (i + 1) * 512].rearrange("p (h j) -> p h j", h=8)
    on128 = P.sb("on128", [64, 128]); P.v("dve", "memset", on128[:], [], 1.0)
    S = P.sb("S", [128, 8, 128]); P.v("dve", "memset", S[:], [], 0.0)
    qTv = qT.rearrange("(h p) t -> p h t", p=128); kTv = kT.rearrange("(h p) t -> p h t", p=128)
    kv = k_tm.rearrange("(c p) f -> p c f", p=64); vv = v_tm.rearrange("(c p) f -> p c f", p=64)
    gv = g_tm.rearrange("(c p) f -> p c f", p=64); bv = b_tm.rearrange("(c p) f -> p c f", p=64)
    ov = o_tm.rearrange("(c p) f -> p c f", p=64)
    NB = 2
    qTs = [P.sb(f"qTs{i}", [128, 8, G2 * 64]) for i in range(NB)]
    kTs = [P.sb(f"kTs{i}", [128, 8, G2 * 64]) for i in range(NB)]
    ks = [P.sb(f"ks{i}", [64, G2, 1024]) for i in range(NB)]
    vs = [P.sb(f"vs{i}", [64, G2, 1024]) for i in range(NB)]
    gs = [P.sb(f"gs{i}", [64, G2, 8]) for i in range(NB)]
    bs = [P.sb(f"bs{i}", [64, G2, 8]) for i in range(NB)]
    os_ = [P.sb(f"os{i}", [64, G2, 1024]) for i in range(NB)]
    pss = psum_pool(P, 4, "pss")
    psb = Rot([P.ps(f"psb{i}", [128, 1024]) for i in range(2)])

    def bc(ap, n):
        return ap.unsqueeze(2).to_broadcast([ap.shape[0], 8, n])

    def mkset(par):
        def R(name, shape, n):
            return Rot([P.sb(f"{name}_{par}_{i}", shape) for i in range(n)])
        B = {}
        for nm in ("a", "kdec", "ba", "nb"):
            B[nm] = R(nm, [64, 8], 1)
        B["dl"] = R("dl", [128, 8], 1)
        for nm in ("G3", "UG3", "t1", "nbSM", "dA", "D3", "DT3"):
            B[nm] = R(nm, [64, 8, 64], 1)
        for nm in ("X", "XT", "TT"):
            B[nm] = R(nm, [64, 8, 64], 3)
        B["aqk"] = R("aqk", [128, 8, 64], 1)
        for nm in ("vb", "kba", "kd", "u"):
            B[nm] = R(nm, [64, 8, 128], 1)
        B["wT"] = R("wT", [128, 8, 64], 1); B["qd"] = R("qd", [128, 8, 64], 1)
        B["vn"] = R("vn", [128, 8, 128], 1)
        for nm in ("aqk", "vn"):
            for t_ in B[nm].items:
                P.v("dve", "memset", t_[:], [], 0.0)
        return B

    BS = [mkset(0), mkset(1)]
    sd = P.sb("sd", [128, 8, 128])
    H = range(8)
    r3 = lambda ap: ap.rearrange("p (h j) -> p h j", h=8)

    def prep(b, ci, par):
        B = BS[par]
        g_c = gs[b][:, ci, :]; be_c = bs[b][:, ci, :]
        kTc = kTs[b][:, :, ci * 64:(ci + 1) * 64]; qTc = qTs[b][:, :, ci * 64:(ci + 1) * 64]
        k_c = ks[b][:, ci, :].rearrange("p (h d) -> p h d", h=8); v_c = vs[b][:, ci, :].rearrange("p (h d) -> p h d", h=8)
        pg = pss.get()
        P.mm(pg[:64, 0:8], C["U"], g_c)
        P.mm(pg[:64, 8:16], C["SU"], g_c)
        P.mm(pg[:, 16:24], on128[:], g_c)
        a = B["a"].get(); kdec = B["kdec"].get(); ba = B["ba"].get(); nb = B["nb"].get(); dl = B["dl"].get()
        P.act(a[:], pg[:64, 0:8], AF.Exp)
        P.act(kdec[:], pg[:64, 8:16], AF.Exp)
        P.act(dl[:], pg[:, 16:24], AF.Exp)
        P.v("dve", "tensor_tensor", ba[:], [be_c, a[:]], ALU.mult)
        P.v("dve", "tensor_scalar", nb[:], [be_c], -1.0, None, ALU.mult)
        G3 = B["G3"].get(); UG3 = B["UG3"].get()
        P.v("dve", "tensor_tensor", G3[:], [C["ON3"], bc(g_c, 64)], ALU.mult)
        P.v("dve", "tensor_tensor", UG3[:], [C["U3"], bc(g_c, 64)], ALU.mult)
        yield
        pD = pss.get()
        for h in H:
            hs = slice(h * 64, (h + 1) * 64)
            P.mm(pD[:64, hs], G3[:, h, :], C["NU"], start=True, stop=False)
            P.mm(pD[:64, hs], UG3[:, h, :], C["ON"], start=False, stop=False)
            P.mm(pD[:64, hs], C["ID"], C["NEG"], start=False, stop=True)
        D3 = B["D3"].get()
        P.act(D3[:], r3(pD[:64, :]), AF.Exp)
        yield
        pDT = pss.get()
        for h in H:
            hs = slice(h * 64, (h + 1) * 64)
            P.mm(pDT[:64, hs], G3[:, h, :], C["U"], start=True, stop=False)
            P.mm(pDT[:64, hs], UG3[:, h, :], C["NON"], start=False, stop=False)
            P.mm(pDT[:64, hs], C["ID"], C["NEGT"], start=False, stop=True)
        DT3 = B["DT3"].get()
        P.act(DT3[:], r3(pDT[:64, :]), AF.Exp)
        yield
        pK = pss.get()
        for h in H:
            P.mm(pK[:64, h * 64:(h + 1) * 64], kTc[:, h, :], kTc[:, h, :])
        t1 = B["t1"].get(); nbSM = B["nbSM"].get(); X = B["X"].get()
        P.v("dve", "tensor_tensor", t1[:], [r3(pK[:64, :]), D3[:]], ALU.mult)
        P.v("dve", "tensor_tensor", nbSM[:], [C["SM3"], bc(nb[:], 64)], ALU.mult)
        P.v("dve", "tensor_tensor", X[:], [t1[:], nbSM[:]], ALU.mult)
        yield
        pKQ = pss.get()
        for h in H:
            P.mm(pKQ[:64, h * 64:(h + 1) * 64], kTc[:, h, :], qTc[:, h, :])
        aqk = B["aqk"].get()
        P.v("dve", "tensor_tensor", aqk[:64], [r3(pKQ[:64, :]), DT3[:]], ALU.mult)
        yield
        pT = pss.get()
        for h in H:
            P.mm(pT[:64, h * 64:(h + 1) * 64], X[:, h, :], C["ID"])
        XT = B["XT"].get(); TT = B["TT"].get()
        P.act(XT[:], r3(pT[:64, :]), AF.Copy)
        P.v("dve", "tensor_tensor", TT[:], [XT[:], C["ID3"]], ALU.add)
        yield
        for lv in range(5):
            pX = pss.get()
            for h in H:
                P.mm(pX[:64, h * 64:(h + 1) * 64], XT[:, h, :], X[:, h, :])
            X2 = B["X"].get()
            P.act(X2[:], r3(pX[:64, :]), AF.Copy)
            yield
            if lv < 4:
                pXT = pss.get()
                for h in H:
                    P.mm(pXT[:64, h * 64:(h + 1) * 64], X[:, h, :], XT[:, h, :])
                XT2 = B["XT"].get()
                P.act(XT2[:], r3(pXT[:64, :]), AF.Copy)
                XT = XT2
                yield
            X = X2
            pTT = pss.get()
            for h in H:
                P.mm(pTT[:64, h * 64:(h + 1) * 64], X[:, h, :], TT[:, h, :])
            TT2 = B["TT"].get()
            P.v("dve", "tensor_tensor", TT2[:], [r3(pTT[:64, :]), TT[:]], ALU.add)
            TT = TT2
            yield
        vb = B["vb"].get(); kba = B["kba"].get(); kd = B["kd"].get(); dA = B["dA"].get()
        P.v("dve", "tensor_tensor", vb[:], [v_c, bc(be_c, 128)], ALU.mult)
        P.v("pool", "tensor_tensor", kba[:], [k_c, bc(ba[:], 128)], ALU.mult)
        P.v("pool", "tensor_tensor", kd[:], [k_c, bc(kdec[:], 128)], ALU.mult)
        P.v("dve", "tensor_tensor", dA[:], [C["ID3"], bc(a[:], 64)], ALU.mult)
        pU = psb.get()
        for h in H:
            P.mm(pU[:64, h * 128:(h + 1) * 128], TT[:, h, :], vb[:, h, :])
        u = B["u"].get()
        P.act(u[:], pU[:64, :].rearrange("p (h d) -> p h d", h=8), AF.Copy)
        yield
        pW = pss.get()
        for h in H:
            P.mm(pW[:, h * 64:(h + 1) * 64], kba[:, h, :], TT[:, h, :])
        wT = B["wT"].get()
        P.act(wT[:], r3(pW[:, :]), AF.Copy)
        yield
        pA = pss.get()
        for h in H:
            P.mm(pA[:, h * 64:(h + 1) * 64], on128[:], dA[:, h, :])
        qd = B["qd"].get()
        P.v("dve", "tensor_tensor", qd[:], [qTc, r3(pA[:, :])], ALU.mult)
        return dict(u=u, wT=wT, qd=qd, aqk=aqk, kd=kd, dl=dl, vn=B["vn"].get(), b=b, ci=ci)

    def recur(t):
        u, wT, qd, aqk, kd, dl, vn, b, ci = (t[k] for k in ("u", "wT", "qd", "aqk", "kd", "dl", "vn", "b", "ci"))
        pWS = psb.get()
        for h in H:
            P.mm(pWS[:64, h * 128:(h + 1) * 128], wT[:, h, :], S[:, h, :])
        P.v("dve", "tensor_tensor", vn[:64], [u[:], pWS[:64, :].rearrange("p (h d) -> p h d", h=8)], ALU.subtract)
        pO = psb.get()
        for h in H:
            P.mm(pO[:64, h * 128:(h + 1) * 128], qd[:, h, :], S[:, h, :], start=True, stop=False)
            P.mm(pO[:64, h * 128:(h + 1) * 128], aqk[:, h, :], vn[:, h, :], start=False, stop=True)
        P.act(os_[b][:, ci, :], pO[:64, :], AF.Copy)
        pS = psb.get()
        for h in H:
            P.mm(pS[:, h * 128:(h + 1) * 128], kd[:, h, :], vn[:64, h, :])
        P.v("dve", "tensor_tensor", sd[:], [S[:], bc(dl[:], 128)], ALU.mult)
        P.v("dve", "tensor_tensor", S[:], [pS[:, :].rearrange("p (h d) -> p h d", h=8), sd[:]], ALU.add)

    ngrp = (nchunk + G2 - 1) // G2

    def load_group(gi):
        b = gi % NB
        c0 = gi * G2; n = min(G2, nchunk - c0)
        P.dma(qTs[b][:, :, :n * 64], qTv[:, :, c0 * 64:(c0 + n) * 64])
        P.dma(kTs[b][:, :, :n * 64], kTv[:, :, c0 * 64:(c0 + n) * 64])
        P.dma(ks[b][:, :n, :], kv[:, c0:c0 + n, :])
        P.dma(vs[b][:, :n, :], vv[:, c0:c0 + n, :])
        P.dma(gs[b][:, :n, :], gv[:, c0:c0 + n, :])
        P.dma(bs[b][:, :n, :], bv[:, c0:c0 + n, :])

    load_group(0)
    for gi in range(ngrp):
        if gi + 1 < ngrp:
            load_group(gi + 1)
        b = gi % NB
        c0 = gi * G2; n = min(G2, nchunk - c0)
        gens = [prep(b, ci, ci % 2) for ci in range(n)]
        results = [None] * n
        active = list(range(n))
        if not pipeline:
            for idx in active:
                try:
                    while True:
                        next(gens[idx])
                except StopIteration as e:
                    results[idx] = e.value
        else:
            while active:
                for idx in list(active):
                    try:
                        next(gens[idx])
                    except StopIteration as e:
                        results[idx] = e.value
                        active.remove(idx)
        for ci in range(n):
            recur(results[ci])
        P.dma(ov[:, c0:c0 + n, :], os_[b][:, :n, :], is_output=True)
    return P


def build_k3_lru(T=T_ALL, NT=512):
    P = Prog()
    xcT = P.dram("xcT", [1024, T]); wr = P.dram("wr", [1024, 128]); wi = P.dram("wi", [1024, 128])
    tabs = P.dram("tabs", [128, 24])
    hT = P.dram("hT", [1024, T], kind="ExternalOutput")
    tb = P.sb("tb", [128, 24]); P.dma(tb[:], tabs)
    wrs = P.sb("wrs", [128, 8, 128]); wis = P.sb("wis", [128, 8, 128])
    P.dma(wrs[:], wr.rearrange("(g i) j -> i g j", i=128)); P.dma(wis[:], wi.rearrange("(g i) j -> i g j", i=128))
    cl = P.sb("cl", [128, 8]); cl2 = P.sb("cl2", [128, 8])
    P.act(cl[:], tb[:, 16:24], AF.Exp, scale=-1.0)
    P.act(cl[:], cl[:], AF.Ln, bias=1.0)
    P.v("dve", "tensor_scalar", cl2[:], [cl[:]], -16.0, None, ALU.mult)
    P.v("dve", "tensor_scalar", cl[:], [cl[:]], -8.0, None, ALU.mult)
    carry = P.sb("carry", [128, 8]); P.v("dve", "memset", carry[:], [], 0.0)
    xs = Rot([P.sb(f"x{i}", [128, 8, NT]) for i in range(2)])
    hs = Rot([P.sb(f"ho{i}", [128, 8, NT]) for i in range(2)])
    psp = psum_pool(P, 6)
    R = lambda nm, n: Rot([P.sb(f"{nm}{i}", [128, NT]) for i in range(n)])
    r_r, ig_r, a_r, a2_r, b_r = R("r", 2), R("ig", 2), R("a", 2), R("a2", 2), R("b", 2)
    xv = xcT.rearrange("(g p) t -> p g t", p=128); hv = hT.rearrange("(g p) t -> p g t", p=128)
    nt_ = (T + NT - 1) // NT
    for t in range(nt_):
        n = min(NT, T - t * NT)
        x = xs.get(); ho = hs.get()
        P.dma(x[:, :, :n], xv[:, :, t * NT:t * NT + n])
        for g in range(8):
            pr = psp.get(); pi = psp.get()
            P.mm(pr[:, :n], wrs[:, g, :], x[:, g, :n]); P.mm(pi[:, :n], wis[:, g, :], x[:, g, :n])
            r = r_r.get(); ig = ig_r.get(); a = a_r.get(); a2 = a2_r.get(); b = b_r.get()
            P.act(r[:, :n], pr[:, :n], AF.Sigmoid, bias=tb[:, g:g + 1])
            P.act(ig[:, :n], pi[:, :n], AF.Sigmoid, bias=tb[:, 8 + g:9 + g])
            P.act(a[:, :n], r[:, :n], AF.Exp, scale=cl[:, g:g + 1])
            P.act(a2[:, :n], r[:, :n], AF.Exp, scale=cl2[:, g:g + 1])
            P.v("dve", "tensor_scalar", a2[:, :n], [a2[:, :n]], -1.0, 1.0, ALU.mult, ALU.add)
            P.act(a2[:, :n], a2[:, :n], AF.Sqrt)
            P.v("dve", "tensor_tensor", b[:, :n], [ig[:, :n], x[:, g, :n]], ALU.mult)
            P.v("dve", "tensor_tensor", b[:, :n], [b[:, :n], a2[:, :n]], ALU.mult)
            P.v("dve", "tensor_tensor_scan", ho[:, g, :n], [a[:, :n], b[:, :n]], carry[:, g:g + 1], ALU.mult, ALU.add,
                rk=[_k(a), _k(b), ("carry", g)], wk=[(ho.name, g)])
            P.v("dve", "tensor_copy", carry[:, g:g + 1], [ho[:, g, n - 1:n]], rk=[(ho.name, g)], wk=[("carry", g)])
        P.dma(hv[:, :, t * NT:t * NT + n], ho[:, :, :n], is_output=True)
    return P


def build_k3_hg(nchunk=NCHUNK):
    P = Prog()
    T = nchunk * 64
    qT = P.dram("qT", [1024, T]); kT = P.dram("kT", [1024, T])
    k_tm = P.dram("k_tm", [T, 1024]); v_tm = P.dram("v_tm", [T, 1024]); lf_tm = P.dram("lf_tm", [T, 1024])
    o_tm = P.dram("o_tm", [T, 1024], kind="ExternalOutput")
    C = load_consts(P)
    S = P.sb("S", [128, 8, 128]); P.v("dve", "memset", S[:], [], 0.0)
    qTv = qT.rearrange("(h p) t -> p h t", p=128); kTv = kT.rearrange("(h p) t -> p h t", p=128)
    kv = k_tm.rearrange("(c p) f -> p c f", p=64); vv = v_tm.rearrange("(c p) f -> p c f", p=64)
    lv = lf_tm.rearrange("(c p) f -> p c f", p=64); ov = o_tm.rearrange("(c p) f -> p c f", p=64)
    NB = 2
    qTs = [P.sb(f"qTs{i}", [128, 8, GRP * 64]) for i in range(NB)]
    kTs = [P.sb(f"kTs{i}", [128, 8, GRP * 64]) for i in range(NB)]
    ks = [P.sb(f"ks{i}", [64, GRP, 1024]) for i in range(NB)]
    ls = [P.sb(f"ls{i}", [64, GRP, 1024]) for i in range(NB)]
    vs = [P.sb(f"vs{i}", [128, GRP, 1024]) for i in range(NB)]
    os_ = [P.sb(f"os{i}", [64, GRP, 1024]) for i in range(NB)]
    for t_ in vs:
        P.v("dve", "memset", t_[64:128, :, :], [], 0.0, wk=[(t_.name, "pad")])
    psp = psum_pool(P, 8)

    def R(name, shape, n):
        return Rot([P.sb(f"{name}{i}", shape) for i in range(n)])

    ref_r = R("ref", [128, 1], 3); nref_r = R("nref", [128, 1], 3)
    E1_r, E2_r, E3_r = R("E1", [128, 64], 2), R("E2", [128, 64], 2), R("E3", [128, 64], 3)
    E4_r = R("E4", [64, 128], 2)
    qt_r, kt_r, qg_r = R("qt", [128, 64], 2), R("kt", [128, 64], 2), R("qg", [128, 64], 2)
    sc_r = R("sc", [128, 64], 2); kg_r = R("kg", [64, 128], 2); sd_r = R("sd", [128, 128], 2)
    for t_ in sc_r.items:
        P.v("dve", "memset", t_[:], [], 0.0)
    ngrp = (nchunk + GRP - 1) // GRP

    def load_group(gi):
        b = gi % NB
        c0 = gi * GRP; n = min(GRP, nchunk - c0)
        P.dma(qTs[b][:, :, :n * 64], qTv[:, :, c0 * 64:(c0 + n) * 64])
        P.dma(kTs[b][:, :, :n * 64], kTv[:, :, c0 * 64:(c0 + n) * 64])
        P.dma(ks[b][:, :n, :], kv[:, c0:c0 + n, :])
        P.dma(ls[b][:, :n, :], lv[:, c0:c0 + n, :])
        P.dma(vs[b][:64, :n, :], vv[:, c0:c0 + n, :], wk=[(vs[b].name, "data")])

    load_group(0)
    for gi in range(ngrp):
        if gi + 1 < ngrp:
            load_group(gi + 1)
        b = gi % NB
        c0 = gi * GRP; n = min(GRP, nchunk - c0)
        for ci in range(n):
            for h in range(8):
                hs = slice(h * 128, (h + 1) * 128)
                kTh = kTs[b][:, h, ci * 64:(ci + 1) * 64]; qTh = qTs[b][:, h, ci * 64:(ci + 1) * 64]
                k_h = ks[b][:, ci, hs]; lf_h = ls[b][:, ci, hs]
                pG = psp.get()
                P.mm(pG[:, 0:64], lf_h, C["U"])
                P.mm(pG[:64, 64:192], C["SU"], lf_h)
                ref = ref_r.get(); nref = nref_r.get()
                P.act(ref[:], pG[:, 32:33], AF.Copy)
                P.act(nref[:], pG[:, 32:33], AF.Copy, scale=-1.0)
                E1 = E1_r.get(); E2 = E2_r.get(); E3 = E3_r.get(); E4 = E4_r.get()
                P.act(E1[:], pG[:, 0:64], AF.Exp, bias=nref[:, 0:1])
                P.act(E2[:], pG[:, 0:64], AF.Exp, bias=ref[:, 0:1], scale=-1.0)
                P.act(E3[:], pG[:, 0:64], AF.Exp)
                P.act(E4[:], pG[:64, 64:192], AF.Exp)
                qt = qt_r.get(); kt = kt_r.get(); qg = qg_r.get(); kg = kg_r.get()
                P.v("dve", "tensor_tensor", qt[:], [qTh, E1[:]], ALU.mult)
                P.v("dve", "tensor_tensor", kt[:], [kTh, E2[:]], ALU.mult)
                P.v("pool", "tensor_tensor", qg[:], [qTh, E3[:]], ALU.mult)
                P.v("pool", "tensor_tensor", kg[:], [k_h, E4[:]], ALU.mult)
                pS = psp.get()
                P.mm(pS[:64, 0:64], kt[:], qt[:])
                sc = sc_r.get()
                P.v("dve", "tensor_tensor", sc[:64, :], [pS[:64, 0:64], C["IM"]], ALU.mult)
                Sh = S[:, h, :]
                pR = psp.get()
                P.mm(pR[:64, 0:128], sc[:], vs[b][:, ci, hs], start=True, stop=False,
                     rk=[_k(sc), (vs[b].name, "pad"), (vs[b].name, "data")])
                P.mm(pR[:64, 0:128], qg[:], Sh, start=False, stop=True, rk=[_k(qg), ("S", h)])
                pR2 = psp.get()
                P.mm(pR2[:, 128:256], kg[:], vs[b][:64, ci, hs], rk=[_k(kg), (vs[b].name, "data")])
                P.act(os_[b][:, ci, hs], pR[:64, 0:128], AF.Copy, wk=[(os_[b].name, (ci, h))])
                sd = sd_r.get()
                P.act(sd[:], Sh, AF.Copy, scale=E3[:, 63:64], rk=[("S", h), _k(E3)])
                P.v("dve", "tensor_tensor", Sh, [pR2[:, 128:256], sd[:]], ALU.add, rk=[_k(pR2), _k(sd)], wk=[("S", h)])
        P.dma(ov[:, c0:c0 + n, :], os_[b][:, :n, :], is_output=True)
    return P


def build_k4a(variant, ntiles_ctx, ntiles_lat, NT=256):
    P = Prog()
    ntok = (ntiles_ctx + ntiles_lat) * NT
    oaT = P.dram("oaT", [1024, ntok]); obT = P.dram("obT", [1024, ntok]); zT = P.dram("zT", [1024, ntok])
    xT = P.dram("xT", [1024, ntok]); wout = P.dram("wout", [1024, 1024])
    tabs = P.dram("tabs", [128, 97])
    outT = P.dram("outT", [1024, ntok], kind="ExternalOutput")
    tb = P.sb("tb", [128, 97]); P.dma(tb[:], tabs)
    ones = make_const(P, "ones", 1.0)
    Wo = load_w_bf16(P, "Wo", wout, 8, 1024)
    NB = 2
    oas = [P.sb(f"oa{i}", [128, 8, NT]) for i in range(NB)]
    obs_ = [P.sb(f"ob{i}", [128, 8, NT]) for i in range(NB)]
    zs = [P.sb(f"z{i}", [128, 8, NT]) for i in range(NB)]
    xs = [P.sb(f"x{i}", [128, 8, NT]) for i in range(NB)]
    xo = P.sb("xo", [128, 8, NT])
    sq = P.sb("sq", [128, 8, NT], BF16)
    y = P.sb("y", [128, 8, NT], BF16)
    rs_r = Rot([P.sb(f"rs{i}", [128, NT]) for i in range(2)])
    y1_r = Rot([P.sb(f"y1{i}", [128, NT]) for i in range(2)])
    psp = psum_pool(P, 6, w=NT)
    views = [a.rearrange("(k p) t -> p k t", p=128) for a in (oaT, obT, zT, xT)]
    ov = outT.rearrange("(k p) t -> p k t", p=128)
    tiles = [(i, 1) for i in range(ntiles_ctx)] + [(ntiles_ctx + i, 0) for i in range(ntiles_lat)]

    def load(ti):
        b = ti % NB; t = tiles[ti][0]
        for buf, vw in zip((oas[b], obs_[b], zs[b], xs[b]), views):
            P.dma(buf[:], vw[:, :, t * NT:(t + 1) * NT])

    load(0)
    for ti, (t, s) in enumerate(tiles):
        if ti + 1 < len(tiles):
            load(ti + 1)
        b = ti % NB
        oa, ob, z, x = oas[b], obs_[b], zs[b], xs[b]
        base = s * 48
        P.v("pool", "tensor_tensor", oa[:], [oa[:], ob[:]], ALU.add)
        if variant == "norm":
            P.act(sq[:], oa[:], AF.Square)
            P.act(z[:], z[:], AF.Silu)
            for h in range(8):
                ps = psp.get()
                P.mm(ps[:, :NT], ones[:], sq[:, h, :])
                rs = rs_r.get(); y1 = y1_r.get()
                P.act(rs[:], ps[:, :NT], AF.Sqrt, scale=1.0 / 128.0, bias=EPS)
                P.v("dve", "reciprocal", rs[:], [rs[:]])
                P.v("dve", "tensor_tensor", y1[:], [oa[:, h, :], rs[:]], ALU.mult)
                P.v("dve", "scalar_tensor_tensor", y[:, h, :], [y1[:]], tb[:, 96:97], z[:, h, :], ALU.mult, ALU.mult,
                    wk=[("y", h)])
        else:
            for h in range(8):
                P.v("dve", "tensor_tensor", y[:, h, :], [z[:, h, :], oa[:, h, :]], ALU.mult, wk=[("y", h)])
        for mo in range(8):
            py = psp.get()
            for k in range(8):
                P.mm(py[:, :NT], Wo[:, k, mo * 128:(mo + 1) * 128], y[:, k, :], start=(k == 0), stop=(k == 7),
                     rk=[("Wo", k), ("y", k)])
            P.v("dve", "scalar_tensor_tensor", xo[:, mo, :], [py[:, :NT]], tb[:, base + 16 + mo:base + 17 + mo], x[:, mo, :],
                ALU.mult, ALU.add, wk=[("xo", mo)])
        P.dma(ov[:, :, t * NT:(t + 1) * NT], xo[:], is_output=True)
    return P


_NC = {}


def get_nc(key, builder):
    if key not in _NC:
        _NC[key] = builder().finish()
    return _NC[key]


def run_nc(nc, maps):
    return run_bass_kernel_spmd(nc, maps, core_ids=list(range(NCORE))).results


def to_scan(a, col):
    if not col:
        return a
    return a.reshape(128, 64, -1).transpose(1, 0, 2).reshape(8192, -1)


def from_scan(a, col):
    if not col:
        return a
    return a.reshape(64, 128, -1).transpose(1, 0, 2).reshape(8192, -1)


def k2_input(seq_ctx, seq_lat, hf, conv):
    cols = []
    for (ioff, nc_, ooff, n, isctx, hl, hr) in k2_tiles(conv):
        seq = seq_ctx if isctx else seq_lat
        base = hf * 128 + ooff if isctx else hf * 4096 + (ooff - 128)
        lo, hi = (base - 2, base + n + 1) if conv else (base, base + n)
        blk = np.zeros((hi - lo, 1024), np.float32)
        a, b_ = max(lo, 0), min(hi, len(seq))
        blk[a - lo:b_ - lo] = seq[a:b_]
        cols.append(blk)
    return np.ascontiguousarray(np.concatenate(cols, 0).T)


def seq_dir(pc, pl, d):
    if d:
        pc, pl = pc[:, ::-1], pl[:, ::-1]
    return np.ascontiguousarray(np.concatenate([pc, pl], 1))


def run_layer(i, XL, XC, mod, inp):
    kind, j, col, last = i % 3, i // 3, i % 2 == 1, i == 3
    kname = ("gdn", "lru", "hg")[kind]
    conv = kname != "hg"
    nc2 = get_nc(("k2", kname, i if kname == "hg" else 0), lambda: build_k2(kname, hg_layer=i))
    maps = []
    for c in range(NCORE):
        b, hf = c // 2, c % 2
        m = {"xT": k2_input(XC[b], to_scan(XL[b], col), hf, conv),
             "tabs": np.ascontiguousarray(np.concatenate([mod_table(mod, i, b), mod_table(mod, i, 4),
                                                          vec_table(inp["norm_mix"][i])], 1)),
             "hmask": np.ascontiguousarray(np.tile(np.array([[hf, 1 - hf, hf, 1 - hf]], np.float32), (128, 1)))}
        if kname == "gdn":
            m["w"] = inp["gdn_w_in"][j]
            m["convw"] = np.ascontiguousarray(inp["gdn_conv"][j].reshape(4, 24, 128).transpose(2, 0, 1).reshape(128, 96))
            m["ident"] = np.eye(128, dtype=np.float32)
            ab = np.zeros((32, 2), np.float32)
            ab[:16, 0] = inp["gdn_dt_bias"][j].reshape(16); ab[:16, 1] = inp["gdn_a_log"][j].reshape(16)
            m["abtab"] = ab
        elif kname == "lru":
            m["w"] = inp["lru_w_in"][j]
            m["convw"] = np.ascontiguousarray(inp["lru_conv_w"][j].reshape(4, 8, 128).transpose(2, 0, 1).reshape(128, 32))
            m["ident"] = np.eye(128, dtype=np.float32)
            m["convb"] = vec_table(inp["lru_conv_b"][j])
        else:
            m["w"] = inp["hg_w_in"][j]
            m["lbl_in"] = np.ascontiguousarray(inp["hg_lb_logits"].reshape(4, 8, 128).transpose(2, 1, 0).reshape(128, 32))
        maps.append(m)
    res = run_nc(nc2, maps)
    rows = K2_ROWS[kname]
    PC = np.empty((4, rows, 256), np.float32); PL = np.empty((4, rows, 8192), np.float32)
    for c in range(NCORE):
        b, hf = c // 2, c % 2
        o = res[c]["outT"]
        PC[b][:, hf * 128:(hf + 1) * 128] = o[:, :128]
        PL[b][:, hf * 4096:(hf + 1) * 4096] = o[:, 128:]
    del res
    cst = chunk_consts(); gcst = gdn_consts()
    maps = []
    for c in range(NCORE):
        b, d = c // 2, c % 2
        if kname == "gdn":
            qT = seq_dir(PC[b][0:1024], PL[b][0:1024], d); kT = seq_dir(PC[b][1024:2048], PL[b][1024:2048], d)
            vT = seq_dir(PC[b][2048:3072], PL[b][2048:3072], d)
            g = seq_dir(PC[b][4096 + d * 8:4104 + d * 8], PL[b][4096 + d * 8:4104 + d * 8], d)
            be = seq_dir(PC[b][4144 + d * 8:4152 + d * 8], PL[b][4144 + d * 8:4152 + d * 8], d)
            maps.append({"qT": qT, "kT": kT, "k_tm": np.ascontiguousarray(kT.T), "v_tm": np.ascontiguousarray(vT.T),
                         "g_tm": np.ascontiguousarray(g.T), "b_tm": np.ascontiguousarray(be.T), "cst": gcst})
        elif kname == "lru":
            tabs = np.concatenate([vec_table(inp["lru_b_r"][j, d]), vec_table(inp["lru_b_i"][j, d]),
                                   vec_table(inp["lru_lambda"][j, d])], 1)
            maps.append({"xcT": seq_dir(PC[b][1024:2048], PL[b][1024:2048], d),
                         "wr": np.ascontiguousarray(inp["lru_w_r"][j, d].reshape(1024, 128)),
                         "wi": np.ascontiguousarray(inp["lru_w_i"][j, d].reshape(1024, 128)),
                         "tabs": np.ascontiguousarray(tabs)})
        else:
            r0 = 1024 + d * 2048
            qT = seq_dir(PC[b][0:1024], PL[b][0:1024], d); kT = seq_dir(PC[b][r0:r0 + 1024], PL[b][r0:r0 + 1024], d)
            lfT = seq_dir(PC[b][r0 + 1024:r0 + 2048], PL[b][r0 + 1024:r0 + 2048], d)
            vT = seq_dir(PC[b][5120:6144], PL[b][5120:6144], d)
            maps.append({"qT": qT, "kT": kT, "k_tm": np.ascontiguousarray(kT.T), "v_tm": np.ascontiguousarray(vT.T),
                         "lf_tm": np.ascontiguousarray(lfT.T), "cst": cst})
    nc3 = get_nc(("k3", kname), {"gdn": build_k3_gdn, "lru": build_k3_lru, "hg": build_k3_hg}[kname])
    res = run_nc(nc3, maps)
    OC = np.empty((2, 4, 256, 1024), np.float32); OL = np.empty((2, 4, 8192, 1024), np.float32)
    for c in range(NCORE):
        b, d = c // 2, c % 2
        o = res[c]["hT"].T if kname == "lru" else res[c]["o_tm"]
        oc, ol = o[:256], o[256:]
        if d:
            oc, ol = oc[::-1], ol[::-1]
        OC[d, b] = oc
        OL[d, b] = from_scan(ol, col)
    del res
    zr = {"gdn": (3072, 4096), "lru": (0, 1024), "hg": (6144, 7168)}[kname]
    variant = "lru" if kname == "lru" else "norm"
    with_ctx = not last
    nc4 = get_nc(("k4a", variant, with_ctx), lambda: build_k4a(variant, 1 if with_ctx else 0, 16))
    normw = {"gdn": lambda: inp["gdn_norm"][j], "hg": lambda: inp["hg_norm"][j], "lru": lambda: np.ones(128, np.float32)}[kname]()
    wout = {"gdn": "gdn_w_out", "lru": "lru_w_out", "hg": "hg_w_out"}[kname]
    maps = []
    for c in range(NCORE):
        b, hf = c // 2, c % 2
        cs, ls = slice(hf * 128, (hf + 1) * 128), slice(hf * 4096, (hf + 1) * 4096)
        zc = PC[b][zr[0]:zr[1]].T
        zl = from_scan(np.ascontiguousarray(PL[b][zr[0]:zr[1]].T), col)
        tabs = np.concatenate([mod_table(mod, i, b), mod_table(mod, i, 4), np.asarray(normw, np.float32).reshape(128, 1)], 1)
        maps.append({"oaT": pack_tok(OC[0, b][cs], OL[0, b][ls], with_ctx), "obT": pack_tok(OC[1, b][cs], OL[1, b][ls], with_ctx),
                     "zT": pack_tok(zc[cs], zl[ls], with_ctx), "xT": pack_tok(XC[b][cs], XL[b][ls], with_ctx),
                     "wout": inp[wout][j], "tabs": np.ascontiguousarray(tabs)})
    res = run_nc(nc4, maps)
    XL2, XC2 = np.empty_like(XL), XC.copy()
    for c in range(NCORE):
        unpack_tok(res[c]["outT"], XL2, XC2, c, with_ctx)
    return XL2, XC2


def kernel(**inputs):
    inp = {k: np.asarray(v) for k, v in inputs.items()}
    mod = run_k1(inp)
    XL = np.ascontiguousarray(inp["x"], dtype=np.float32)
    XC = np.ascontiguousarray(inp["ctx"], dtype=np.float32)
    for i in range(4):
        XL, XC = run_layer(i, XL, XC, mod, inp)
        XL, XC = run_ffn(XL, XC, mod, i, inp, final=(i == 3))
    return XL
```
